# Optimizing a Trainium2 kernel written in Bass

```python
import jax
import jax.numpy as jnp
from jax import lax
import numpy as np

D_MODEL = 1024
BATCH = 8
SEQ = 4096
DEPTH = 2

GRID_W = 64
CTX_LEN = 256
CHUNK = 64
GLA_HEADS = 4
GLA_DK = 64
GLA_DV = 128
GLA_GATE_RANK = 16
GLA_GATE_TAU = 16.0
ATT_Q_HEADS = 8
ATT_KV_HEADS = 2
ATT_HEAD_DIM = 64
ATT_GROUP = ATT_Q_HEADS // ATT_KV_HEADS
ROPE_AXIS_DIM = ATT_HEAD_DIM // 2
ROPE_THETA = 10000.0
Q_BLOCK = 128
HGRN_HEADS = 8
HGRN_DF = 128
HGRN_DV = D_MODEL // HGRN_HEADS
D_FF = ((8 * D_MODEL + 3 * 256 - 1) // (3 * 256)) * 256

EVEN_SIZES = (GLA_HEADS * GLA_DK, GLA_HEADS * GLA_DK, GLA_HEADS * GLA_DV, GLA_HEADS * GLA_DV, 2 * GLA_GATE_RANK,
              ATT_Q_HEADS * ATT_HEAD_DIM, ATT_KV_HEADS * ATT_HEAD_DIM, ATT_KV_HEADS * ATT_HEAD_DIM)
ODD_SIZES = (HGRN_HEADS * HGRN_DF,) * 3 + (HGRN_HEADS * HGRN_DV,) * 2
EVEN_MIX = GLA_HEADS * GLA_DV + ATT_Q_HEADS * ATT_HEAD_DIM
ODD_MIX = HGRN_HEADS * HGRN_DV

kernel_name = 'hybrid_gla_gqa_hgrn2_prefix_dit'


def _rms(x, gain, eps=1e-6):
    xf = x.astype(jnp.float32)
    y = xf * lax.rsqrt(jnp.mean(xf * xf, axis=-1, keepdims=True) + eps)
    return (y * gain.astype(jnp.float32)).astype(x.dtype)


def _modulate(h, shift, scale):
    return h * (1 + scale) + shift


def _split(a, sizes):
    return jnp.split(a, np.cumsum(sizes)[:-1].tolist(), axis=-1)


def _heads(a, n_heads):
    B, N, _ = a.shape
    return a.reshape(B, N, n_heads, -1).transpose(0, 2, 1, 3)


def _merge(a):
    B, H, N, d = a.shape
    return a.transpose(0, 2, 1, 3).reshape(B, N, H * d)


def _gated_readout(o, og, gain):
    return _merge(_rms(o, gain) * jax.nn.silu(og))


def _swiglu(h, w_gate, w_up, w_down):
    return (jax.nn.silu(h @ w_gate) * (h @ w_up)) @ w_down


def _chunk_gla(q, k, v, g, s0):
    B, H, T, K = q.shape
    V = v.shape[-1]
    n = T // CHUNK

    def blocks(a):
        return jnp.moveaxis(a.astype(jnp.float32).reshape(B, H, n, CHUNK, a.shape[-1]), 2, 0)

    tri = jnp.tril(jnp.ones((CHUNK, CHUNK), dtype=bool))[:, :, None]

    def step(s, blk):
        qc, kc, vc, gc = blk
        G = jnp.cumsum(gc, axis=2)
        decay = jnp.exp(jnp.where(tri, G[:, :, :, None, :] - G[:, :, None, :, :], -jnp.inf))
        scores = jnp.einsum('bhik,bhjk,bhijk->bhij', qc, kc, decay)
        o = jnp.einsum('bhij,bhjv->bhiv', scores, vc) + jnp.einsum('bhik,bhkv->bhiv', qc * jnp.exp(G), s)
        G_last = G[:, :, -1:, :]
        s = s * jnp.exp(G_last)[:, :, 0, :, None] + jnp.einsum('bhjk,bhjv->bhkv', kc * jnp.exp(G_last - G), vc)
        return s, o

    s, o = lax.scan(step, s0, (blocks(q), blocks(k), blocks(v), blocks(g)))
    return jnp.moveaxis(o, 0, 2).reshape(B, H, T, V).astype(v.dtype), s


def _directional_scan(q, k, v, g, n_ctx, reverse):
    def seg(a, sl):
        a = a[:, :, sl]
        return jnp.flip(a, 2) if reverse else a
    c_sl, t_sl = slice(0, n_ctx), slice(n_ctx, None)
    B, H, _, K = q.shape
    s0 = jnp.zeros((B, H, K, v.shape[-1]), jnp.float32)
    o_ctx, s_ctx = _chunk_gla(seg(q, c_sl), seg(k, c_sl), seg(v, c_sl), seg(g, c_sl), s0)
    o_lat, _ = _chunk_gla(seg(q, t_sl), seg(k, t_sl), seg(v, t_sl), seg(g, t_sl), s_ctx)
    if reverse:
        o_ctx, o_lat = jnp.flip(o_ctx, 2), jnp.flip(o_lat, 2)
    return o_ctx, o_lat


def _axial_rope_tables(n_tokens):
    rows_n = n_tokens // GRID_W
    row = jnp.repeat(jnp.arange(rows_n), GRID_W).astype(jnp.float32)
    col = jnp.tile(jnp.arange(GRID_W), rows_n).astype(jnp.float32)
    inv = ROPE_THETA ** (-jnp.arange(0, ROPE_AXIS_DIM, 2, dtype=jnp.float32) / ROPE_AXIS_DIM)
    ang = jnp.concatenate([row[:, None] * inv, col[:, None] * inv], axis=-1)
    return jnp.cos(ang), jnp.sin(ang)


def _rope(x, cos, sin):
    xf = x.astype(jnp.float32).reshape(x.shape[:-1] + (-1, 2))
    x0, x1 = xf[..., 0], xf[..., 1]
    out = jnp.stack([x0 * cos - x1 * sin, x0 * sin + x1 * cos], axis=-1)
    return out.reshape(x.shape).astype(x.dtype)


def _softmax_attend(q, k, v):
    s = jnp.einsum('bhgqd,bhkd->bhgqk', q, k, preferred_element_type=jnp.float32) * (ATT_HEAD_DIM ** -0.5)
    p = jax.nn.softmax(s, axis=-1).astype(v.dtype)
    return jnp.einsum('bhgqk,bhkd->bhgqd', p, v)


def _even_mixer(h_ctx, h_lat, w_in, gla_w_gate, gla_b_gate, gla_out_norm, att_q_norm, att_k_norm, cos, sin, ctx_out):
    B, n_ctx, _ = h_ctx.shape
    h = jnp.concatenate([h_ctx, h_lat], axis=1)
    N = h.shape[1]
    T = N - n_ctx
    gq, gk, gv, gog, gz, aq, ak, av = _split(h @ w_in, EVEN_SIZES)

    q = _heads(gq, GLA_HEADS) * (GLA_DK ** -0.5)
    k = _heads(gk, GLA_HEADS)
    v = _heads(gv, GLA_HEADS)
    og = _heads(gog, GLA_HEADS)
    dirs = []
    for d, z in enumerate(jnp.split(gz, 2, axis=-1)):
        logit = (z @ gla_w_gate[d] + gla_b_gate[d]).astype(jnp.float32)
        g = _heads(jax.nn.log_sigmoid(logit) / GLA_GATE_TAU, GLA_HEADS)
        dirs.append(_directional_scan(q, k, v, g, n_ctx, reverse=(d == 1)))
    a_lat = _gated_readout(dirs[0][1] + dirs[1][1], og[:, :, n_ctx:], gla_out_norm)

    qa = _rms(aq.reshape(B, N, ATT_KV_HEADS, ATT_GROUP, ATT_HEAD_DIM), att_q_norm).transpose(0, 2, 3, 1, 4)
    ka = _rms(ak.reshape(B, N, ATT_KV_HEADS, ATT_HEAD_DIM), att_k_norm).transpose(0, 2, 1, 3)
    va = av.reshape(B, N, ATT_KV_HEADS, ATT_HEAD_DIM).transpose(0, 2, 1, 3)
    q_lat = _rope(qa[:, :, :, n_ctx:], cos, sin)
    k_all = jnp.concatenate([ka[:, :, :n_ctx], _rope(ka[:, :, n_ctx:], cos, sin)], axis=2)
    q_blocks = jnp.moveaxis(q_lat.reshape(B, ATT_KV_HEADS, ATT_GROUP, T // Q_BLOCK, Q_BLOCK, ATT_HEAD_DIM), 3, 0)
    o_blocks = lax.map(lambda qb: _softmax_attend(qb, k_all, va), q_blocks)
    b_lat = jnp.moveaxis(o_blocks, 0, 3).reshape(B, ATT_KV_HEADS, ATT_GROUP, T, ATT_HEAD_DIM)
    b_lat = b_lat.transpose(0, 3, 1, 2, 4).reshape(B, T, ATT_Q_HEADS * ATT_HEAD_DIM)
    o_lat = jnp.concatenate([a_lat, b_lat], axis=-1)
    if not ctx_out:
        return None, o_lat
    a_ctx = _gated_readout(dirs[0][0] + dirs[1][0], og[:, :, :n_ctx], gla_out_norm)
    b_ctx = _softmax_attend(qa[:, :, :, :n_ctx], ka[:, :, :n_ctx], va[:, :, :n_ctx])
    b_ctx = b_ctx.transpose(0, 3, 1, 2, 4).reshape(B, n_ctx, ATT_Q_HEADS * ATT_HEAD_DIM)
    return jnp.concatenate([a_ctx, b_ctx], axis=-1), o_lat


def _odd_mixer(h_ctx, h_lat, w_in, lower_bounds, layer, out_norm, ctx_out):
    n_ctx = h_ctx.shape[1]
    h = jnp.concatenate([h_ctx, h_lat], axis=1)
    fq, f_fwd, f_bwd, fi, fog = _split(h @ w_in, ODD_SIZES)
    q = _heads(jax.nn.silu(fq), HGRN_HEADS)
    i = _heads(fi, HGRN_HEADS)
    og = _heads(fog, HGRN_HEADS)
    lbs = jnp.cumsum(jax.nn.softmax(lower_bounds.astype(jnp.float32), axis=1), axis=1)
    lb = (lbs[:, layer] - lbs[:, 0]).reshape(2, HGRN_HEADS, 1, HGRN_DF)
    dirs = []
    for d, f in enumerate((f_fwd, f_bwd)):
        log_f = jnp.logaddexp(jnp.log(lb[d]), jnp.log1p(-lb[d]) + jax.nn.log_sigmoid(_heads(f, HGRN_HEADS).astype(jnp.float32)))
        k = -jnp.expm1(log_f)
        dirs.append(_directional_scan(q, k, i, log_f, n_ctx, reverse=(d == 1)))
    o_lat = _gated_readout(dirs[0][1] + dirs[1][1], og[:, :, n_ctx:], out_norm)
    if not ctx_out:
        return None, o_lat
    return _gated_readout(dirs[0][0] + dirs[1][0], og[:, :, :n_ctx], out_norm), o_lat


def setup_inputs(seed: int = 0) -> dict:
    key = jax.random.key(seed)
    ks = iter(jax.random.split(key, 32))
    n_even, n_odd = (DEPTH + 1) // 2, DEPTH // 2
    D = D_MODEL

    def nrm(shape, scale):
        return scale * jax.random.normal(next(ks), shape, jnp.float32)

    def gain(shape):
        return 1.0 + nrm(shape, 0.02)

    return {
        'x': nrm((BATCH, SEQ, D), 1.0),
        'c': nrm((BATCH, D), 1.0),
        'ctx': nrm((BATCH, CTX_LEN, D), 1.0),
        'c_ctx': nrm((D,), 1.0),
        'mod_w': nrm((DEPTH, D, 6 * D), D ** -0.5),
        'mod_b': nrm((DEPTH, 6 * D), 0.02),
        'norm_pre_mix': gain((DEPTH, D)),
        'norm_post_mix': gain((DEPTH, D)),
        'norm_pre_ffn': gain((DEPTH, D)),
        'norm_post_ffn': gain((DEPTH, D)),
        'even_w_in': nrm((n_even, D, sum(EVEN_SIZES)), D ** -0.5),
        'gla_w_gate': nrm((n_even, 2, GLA_GATE_RANK, GLA_HEADS * GLA_DK), GLA_GATE_RANK ** -0.5),
        'gla_b_gate': nrm((n_even, 2, GLA_HEADS * GLA_DK), 0.02),
        'gla_out_norm': gain((n_even, GLA_DV)),
        'att_q_norm': gain((n_even, ATT_HEAD_DIM)),
        'att_k_norm': gain((n_even, ATT_HEAD_DIM)),
        'even_w_out': nrm((n_even, EVEN_MIX, D), EVEN_MIX ** -0.5),
        'odd_w_in': nrm((n_odd, D, sum(ODD_SIZES)), D ** -0.5),
        'hgrn_lower_bounds': nrm((2, DEPTH, HGRN_HEADS * HGRN_DF), 0.1),
        'hgrn_out_norm': gain((n_odd, HGRN_DV)),
        'odd_w_out': nrm((n_odd, ODD_MIX, D), ODD_MIX ** -0.5),
        'ffn_w_gate': nrm((DEPTH, D, D_FF), D ** -0.5),
        'ffn_w_up': nrm((DEPTH, D, D_FF), D ** -0.5),
        'ffn_w_down': nrm((DEPTH, D_FF, D), D_FF ** -0.5),
    }


def reference(x, c, ctx, c_ctx, mod_w, mod_b, norm_pre_mix, norm_post_mix, norm_pre_ffn, norm_post_ffn,
              even_w_in, gla_w_gate, gla_b_gate, gla_out_norm, att_q_norm, att_k_norm, even_w_out,
              odd_w_in, hgrn_lower_bounds, hgrn_out_norm, odd_w_out, ffn_w_gate, ffn_w_up, ffn_w_down):
    cos, sin = _axial_rope_tables(x.shape[1])
    sc = jax.nn.silu(c)
    scc = jax.nn.silu(c_ctx)
    x_ctx, x_lat = ctx, x
    for l in range(DEPTH):
        last = l == DEPTH - 1
        j = l // 2
        m_lat = jnp.split((sc @ mod_w[l] + mod_b[l])[:, None, :], 6, axis=-1)
        m_ctx = jnp.split((scc @ mod_w[l] + mod_b[l])[None, None, :], 6, axis=-1)
        h_ctx = _modulate(_rms(x_ctx, norm_pre_mix[l]), m_ctx[0], m_ctx[1])
        h_lat = _modulate(_rms(x_lat, norm_pre_mix[l]), m_lat[0], m_lat[1])
        if l % 2 == 0:
            o_ctx, o_lat = _even_mixer(h_ctx, h_lat, even_w_in[j], gla_w_gate[j], gla_b_gate[j], gla_out_norm[j],
                                       att_q_norm[j], att_k_norm[j], cos, sin, not last)
            w_out = even_w_out[j]
        else:
            o_ctx, o_lat = _odd_mixer(h_ctx, h_lat, odd_w_in[j], hgrn_lower_bounds, l, hgrn_out_norm[j], not last)
            w_out = odd_w_out[j]
        x_lat = x_lat + m_lat[2] * _rms(o_lat @ w_out, norm_post_mix[l])
        h_lat = _modulate(_rms(x_lat, norm_pre_ffn[l]), m_lat[3], m_lat[4])
        x_lat = x_lat + m_lat[5] * _rms(_swiglu(h_lat, ffn_w_gate[l], ffn_w_up[l], ffn_w_down[l]), norm_post_ffn[l])
        if not last:
            x_ctx = x_ctx + m_ctx[2] * _rms(o_ctx @ w_out, norm_post_mix[l])
            h_ctx = _modulate(_rms(x_ctx, norm_pre_ffn[l]), m_ctx[3], m_ctx[4])
            x_ctx = x_ctx + m_ctx[5] * _rms(_swiglu(h_ctx, ffn_w_gate[l], ffn_w_up[l], ffn_w_down[l]), norm_post_ffn[l])
    return x_lat
```

```python
from contextlib import ExitStack, contextmanager
import numpy as np
import concourse.bass as bass
import concourse.mybir as mybir
from concourse.bass_utils import run_bass_kernel_spmd

F32 = mybir.dt.float32
BF16 = mybir.dt.bfloat16
AF = mybir.ActivationFunctionType
ALU = mybir.AluOpType
AX = mybir.AxisListType

D = 1024
SEQ = 4096
NCTX = 256
NTOK = SEQ + NCTX
NT = NTOK // 128
DFF = 2816
NFF = DFF // 128
EPS = 1e-6
F0 = 2336
F1 = 5120

ENGS = ['pe', 'act', 'dve', 'pool', 'sp']
NDS = 12
SAME_ENGINE_SYNC = True


class Buf:
    __slots__ = ('name', 'w', 'rs', 'excl')

    def __init__(self, name=''):
        self.name = name
        self.w = None
        self.rs = {}
        self.excl = False


class T:
    def __init__(self, t, name=''):
        self.t = t
        self.b = Buf(name)

    def __getitem__(self, k):
        return self.t[k]


def _bufs(lst):
    out = []
    for x in lst:
        if x is None:
            continue
        out.append(x.b if isinstance(x, T) else x)
    return out


class Sched:
    def __init__(self, nc, stack):
        self.nc = nc
        self.sem = {e: stack.enter_context(nc.semaphore("s_" + e)) for e in ENGS}
        self.cnt = {e: 0 for e in ENGS}
        self.waited = {e: {} for e in ENGS}
        self.prog = {e: [] for e in ENGS}
        self.dsem = {q: [stack.enter_context(nc.semaphore("d_%s%d" % (q, i))) for i in range(NDS)]
                     for q in ('sp', 'pool')}
        self.dtot = {q: [0] * NDS for q in ('sp', 'pool')}
        self.dnext = {q: 0 for q in ('sp', 'pool')}
        self.st = None
        self.nalloc = 0
        self.rec = None

    def _semof(self, key):
        if key[0] == 'e':
            return self.sem[key[1]]
        return self.dsem[key[1]][key[2]]

    def _deps(self, r, w, eng=None):
        toks = {}
        same = ('e', eng)

        def add(tok, raw=False):
            if tok is None:
                return
            k, v = tok
            if k == same and not raw and eng != 'pool':
                return
            if toks.get(k, 0) < v:
                toks[k] = v
        for b in r:
            add(b.w, True)
            if b.excl:
                for k, v in b.rs.items():
                    add((k, v))
        for b in w:
            add(b.w)
            for k, v in b.rs.items():
                add((k, v))
        return toks

    def _waits(self, eng, toks):
        waits = []
        wd = self.waited[eng]
        for k, v in toks.items():
            if k == ('e', eng):
                if eng in ('pe', 'sp') or not SAME_ENGINE_SYNC:
                    continue
            if wd.get(k, 0) >= v:
                continue
            wd[k] = v
            waits.append((k, v))
        return waits

    def _commit(self, tok, r, w):
        k, v = tok
        for b in w:
            b.w = tok
            b.rs = {}
        for b in r:
            if b in w:
                continue
            if b.rs.get(k, 0) < v:
                b.rs[k] = v

    def record(self, f, *a):
        old = self.rec
        self.rec = []
        f(*a)
        out = self.rec
        self.rec = old
        return out

    def replay_zip(self, a, b):
        out = []
        for i in range(max(len(a), len(b))):
            if i < len(a):
                out.append(a[i])
            if i < len(b):
                out.append(b[i])
        self.replay(out)

    def replay(self, lst):
        for it in lst:
            if it[0] == 'op':
                self.op(it[1], it[2], it[3], it[4])
            else:
                self.dma(it[1], it[2], it[3], r=it[4], w=it[5], **it[6])

    def op(self, eng, fn, r=(), w=()):
        if self.rec is not None:
            self.rec.append(('op', eng, fn, r, w))
            return
        r = _bufs(r)
        w = _bufs(w)
        waits = self._waits(eng, self._deps(r, w, eng))
        self.cnt[eng] += 1
        tok = (('e', eng), self.cnt[eng])
        self.prog[eng].append((waits, fn, None))
        self._commit(tok, r, w)

    def dma(self, q, out, in_, r=(), w=(), **kw):
        if self.rec is not None:
            self.rec.append(('dma', q, out, in_, r, w, kw))
            return
        r = _bufs(r)
        w = _bufs(w)
        toks = self._deps(r, w)
        i = self.dnext[q]
        self.dnext[q] = (i + 1) % NDS
        key = ('d', q, i)
        if self.dtot[q][i]:
            if toks.get(key, 0) < self.dtot[q][i]:
                toks[key] = self.dtot[q][i]
        waits = self._waits(q, toks)
        self.dtot[q][i] += 16
        tok = (key, self.dtot[q][i])

        def fn(e, out=out, in_=in_, kw=kw):
            return e.dma_start(out=out, in_=in_, **kw)
        self.prog[q].append((waits, fn, i))
        self._commit(tok, r, w)

    def mm(self, out, lhsT, rhs, start, stop, r, w):
        self.op('pe', lambda e: e.matmul(out, lhsT, rhs, start=start, stop=stop), r, w)

    def tr(self, out, in_, ident, r, w):
        self.op('pe', lambda e: e.transpose(out, in_, ident), r, w)

    def act(self, out, in_, func, r, w, bias=None, scale=None):
        kw = {}
        if bias is not None:
            kw['bias'] = bias
        if scale is not None:
            kw['scale'] = scale
        self.op('act', lambda e: e.activation(out, in_, func, **kw), r, w)

    def tt(self, eng, out, in0, in1, op, r, w):
        self.op(eng, lambda e: e.tensor_tensor(out, in0, in1, op), r, w)

    def ts(self, eng, out, in0, s1, s2, op0, op1, r, w):
        if op1 is None:
            self.op(eng, lambda e: e.tensor_scalar(out, in0, s1, None, op0), r, w)
        else:
            self.op(eng, lambda e: e.tensor_scalar(out, in0, s1, s2, op0, op1), r, w)

    def stt(self, out, in0, scalar, in1, op0, op1, r, w):
        self.op('dve', lambda e: e.scalar_tensor_tensor(out, in0, scalar, in1, op0, op1), r, w)

    def copy(self, eng, out, in_, r, w):
        if eng == 'act':
            self.op('act', lambda e: e.activation(out, in_, AF.Copy), r, w)
        else:
            self.op(eng, lambda e: e.tensor_copy(out, in_), r, w)

    def sb(self, shape, dtype, name=None, stack=None):
        self.nalloc += 1
        name = "%s_%d" % (name or 'sb', self.nalloc)
        return T((stack or self.st).enter_context(self.nc.sbuf_tensor(name, list(shape), dtype)), name)

    def ps(self, shape, dtype, name=None):
        self.nalloc += 1
        name = "%s_%d" % (name or 'ps', self.nalloc)
        t = T(self.st.enter_context(self.nc.psum_tensor(name, list(shape), dtype)), name)
        t.b.excl = True
        return t

    def _emit(self, eng, e):
        for waits, fn, di in self.prog[eng]:
            for k, v in waits:
                e.wait_ge(self._semof(k), v)
            ins = fn(e)
            if di is None:
                ins.then_inc(self.sem[eng], 1)
            else:
                ins.then_inc(self.dsem[eng][di], 16)
        self.prog[eng] = []

    def _drain(self):
        for q in ('sp', 'pool'):
            toks = {}
            for i in range(NDS):
                if self.dtot[q][i]:
                    toks[('d', q, i)] = self.dtot[q][i]
            waits = self._waits(q, toks)
            if waits:
                self.prog[q].append((waits, None, 'drain'))

    def _emit2(self, eng, e):
        for waits, fn, di in self.prog[eng]:
            for k, v in waits:
                e.wait_ge(self._semof(k), v)
            if fn is None:
                continue
            ins = fn(e)
            if di is None:
                ins.then_inc(self.sem[eng], 1)
            else:
                ins.then_inc(self.dsem[eng][di], 16)
        self.prog[eng] = []

    @contextmanager
    def phase(self, name):
        with ExitStack() as st:
            self.st = st
            yield
            self._drain()
            with self.nc.Block() as block:
                block.tensor(lambda e: self._emit2('pe', e))
                block.scalar(lambda e: self._emit2('act', e))
                block.vector(lambda e: self._emit2('dve', e))
                block.gpsimd(lambda e: self._emit2('pool', e))
                block.sync(lambda e: self._emit2('sp', e))
            self.st = None


class Rot:
    def __init__(self, items):
        self.items = items
        self.i = 0

    def next(self):
        x = self.items[self.i]
        self.i = (self.i + 1) % len(self.items)
        return x


def bcast_row(ap_row, n=128):
    return ap_row.broadcast_to([n, ap_row.shape[-1]])


def host_consts():
    c = {}
    c['ident'] = np.eye(128, dtype=np.float32)
    j = np.arange(128)[:, None]
    i = np.arange(128)[None, :]
    same = (j // 64) == (i // 64)
    c['ufwd'] = (same & (j <= i)).astype(np.float32)
    c['ubwd'] = (same & (j >= i)).astype(np.float32)
    c['mfwd'] = (same & (j <= i)).astype(np.float32)
    c['mbwd'] = (same & (j >= i)).astype(np.float32)
    t = np.arange(SEQ)
    row = (t // 64).astype(np.float32)
    col = (t % 64).astype(np.float32)
    inv = (10000.0 ** (-np.arange(0, 32, 2, dtype=np.float32) / 32)).astype(np.float32)
    ang = np.concatenate([row[:, None] * inv, col[:, None] * inv], axis=-1).astype(np.float32)
    c['cos'] = np.cos(ang).astype(np.float32)
    c['sin'] = np.sin(ang).astype(np.float32)
    return c


WEIGHT_NAMES = ['mod_w', 'mod_b', 'norm_pre_mix', 'norm_post_mix', 'norm_pre_ffn', 'norm_post_ffn',
                'even_w_in', 'gla_w_gate', 'gla_b_gate', 'gla_out_norm', 'att_q_norm', 'att_k_norm',
                'even_w_out', 'odd_w_in', 'hgrn_lower_bounds', 'hgrn_out_norm', 'odd_w_out',
                'ffn_w_gate', 'ffn_w_up', 'ffn_w_down']
WEIGHT_SHAPES = {
    'mod_w': [2, D, 6 * D], 'mod_b': [2, 6 * D], 'norm_pre_mix': [2, D], 'norm_post_mix': [2, D],
    'norm_pre_ffn': [2, D], 'norm_post_ffn': [2, D], 'even_w_in': [1, D, F0],
    'gla_w_gate': [1, 2, 16, 256], 'gla_b_gate': [1, 2, 256], 'gla_out_norm': [1, 128],
    'att_q_norm': [1, 64], 'att_k_norm': [1, 64], 'even_w_out': [1, D, D], 'odd_w_in': [1, D, F1],
    'hgrn_lower_bounds': [2, 2, D], 'hgrn_out_norm': [1, 128], 'odd_w_out': [1, D, D],
    'ffn_w_gate': [2, D, DFF], 'ffn_w_up': [2, D, DFF], 'ffn_w_down': [2, DFF, D],
}
CONST_SHAPES = {'ident': [128, 128], 'ufwd': [128, 128], 'ubwd': [128, 128], 'mfwd': [128, 128],
                'mbwd': [128, 128], 'cos': [SEQ, 32], 'sin': [SEQ, 32]}


class Prog:
    def __init__(self, debug=False, upto=99, skip=(), inject=()):
        self.debug = debug
        self.upto = upto
        self.skip = set(skip)
        nc = bass.Bass("TRN2", target_bir_lowering=False)
        self.nc = nc
        self.I = {}
        self.I['xin'] = nc.dram_tensor("xin", [NTOK, D], F32, kind="ExternalInput").ap()
        self.I['cc'] = nc.dram_tensor("cc", [128, 16], F32, kind="ExternalInput").ap()
        for n in WEIGHT_NAMES:
            self.I[n] = nc.dram_tensor(n, WEIGHT_SHAPES[n], F32, kind="ExternalInput").ap()
        for n, s in CONST_SHAPES.items():
            self.I[n] = nc.dram_tensor(n, s, F32, kind="ExternalInput").ap()
        self.out = nc.dram_tensor("out", [SEQ, D], F32, kind="ExternalOutput").ap()
        kind = dict(kind="ExternalOutput") if debug else {}
        self.S = {}

        def scr(name, shape, dt):
            if name in inject:
                self.S[name] = nc.dram_tensor(name, shape, dt, kind="ExternalInput").ap()
            else:
                self.S[name] = nc.dram_tensor(name, shape, dt, **kind).ap()
        scr('MODS', [2, 6, 2, D], F32)
        scr('XR', [NTOK, D], F32)
        scr('P0', [NTOK, F0], BF16)
        scr('P1A', [NTOK, 3072], BF16)
        scr('P1B', [NTOK, 2048], F32)
        scr('OF', [NTOK, D], F32)
        scr('O', [NTOK, D], BF16)
        self.db = {n: [Buf("%s%d" % (n, t)) for t in range(NT)] for n in ['XR', 'P0', 'P1A', 'P1B', 'OF', 'O', 'OUT']}
        self.db_mods = Buf('MODS')

        with ExitStack() as stack:
            self.s = Sched(nc, stack)
            self.build()

    def load_consts(self, names):
        s = self.s
        out = {}
        for n in names:
            t = s.sb([128, 128], F32, n)
            s.dma('sp', t[:], self.I[n][:, :], w=[t])
            out[n] = t
        return out

    def load_ident_bf16(self):
        s = self.s
        t = s.sb([128, 128], BF16, 'identb')
        s.dma('pool', t[:], self.I['ident'][:, :], w=[t])
        return t

    def load_w_bf16(self, wt, src, rows, cols, c0=0, q='pool'):
        s = self.s
        for kc in range(rows // 128):
            s.dma(q, wt[:, kc, 0:cols], src[kc * 128:(kc + 1) * 128, c0:c0 + cols], w=[wt])

    def build(self):
        order = [('mods', self.phase_mods), ('proj0', lambda: self.phase_proj(0)),
                 ('gla0', lambda: self.phase_scan(0)), ('att0', self.phase_att),
                 ('outp0', lambda: self.phase_outproj(0)), ('ffn0', lambda: self.phase_ffn(0)),
                 ('proj1', lambda: self.phase_proj(1)), ('hgrn1', lambda: self.phase_scan(1)),
                 ('outp1', lambda: self.phase_outproj(1)), ('ffn1', lambda: self.phase_ffn(1))]
        self.pre = {}
        names = [n for n, _ in order]
        full = self.upto >= len(order) - 1 and not self.skip
        if full:
            self._build_full(dict(order))
            return
        for i, (name, fn) in enumerate(order):
            if i > self.upto or name in self.skip:
                continue
            if name.startswith('outp') and (i + 1) <= self.upto and ('ffn' + name[-1]) not in self.skip:
                with ExitStack() as pst:
                    l = int(name[-1])
                    self.pre['ffn%d' % l] = dict(
                        wg=self.s.sb([128, 8, DFF], BF16, 'wg', stack=pst),
                        wu=self.s.sb([128, 8, DFF], BF16, 'wu', stack=pst),
                        wd=self.s.sb([128, NFF, D], BF16, 'wd', stack=pst), loaded=False)
                    fn()
                    order[i + 1][1]()
                    self.skip.add(order[i + 1][0])
                continue
            fn()

    def _build_full(self, ph):
        sch = self.s
        I = self.I

        def ffn_w(pst):
            return dict(wg=sch.sb([128, 8, DFF], BF16, 'wg', stack=pst), wu=sch.sb([128, 8, DFF], BF16, 'wu', stack=pst),
                        wd=sch.sb([128, NFF, D], BF16, 'wd', stack=pst), loaded=False)
        with ExitStack() as pst:
            self.pre['proj0'] = dict(w=sch.sb([128, 8, F0], BF16, 'win0', stack=pst), loaded=False)
            ph['mods']()
            ph['proj0']()
        for l, mixers in ((0, ['gla0', 'att0']), (1, ['hgrn1'])):
            if l == 1:
                ph['proj1']()
            for m in mixers:
                ph[m]()
            with ExitStack() as pst:
                self.pre['ffn%d' % l] = ffn_w(pst)
                ph['outp%d' % l]()
                ph['ffn%d' % l]()

    def prefetch(self, key, src, rows, cols):
        pre = self.pre.get(key)
        if pre is not None and not pre['loaded']:
            self.load_w_bf16(pre['w'], src, rows, cols)
            pre['loaded'] = True

    def post_norm_res(self, y, xt, G, res, dst_ap, dst_buf, add_eng='pool'):
        s = self.s
        junk, ss, rstd, tmp = res['junk'], res['ss'].next(), res['rstd'].next(), res['tmp']
        if 'junk2' in res:
            junk, tmp = res['junk2'].next(), res['tmp2'].next()
        s.act(junk[:], y[:], AF.Square, r=[y], w=[junk])
        s.op('dve', lambda e: e.tensor_reduce(ss[:], junk[:], AX.X, ALU.add), r=[junk], w=[ss])
        s.act(ss[:], ss[:], AF.Sqrt, r=[ss, res['eps']], w=[ss], bias=res['eps'][:], scale=1.0 / D)
        s.op('dve', lambda e: e.reciprocal(rstd[:], ss[:]), r=[ss], w=[rstd])
        s.stt(tmp[:], y[:], rstd[:], G[:], ALU.mult, ALU.mult, r=[y, rstd, G], w=[tmp])
        s.tt(add_eng, xt[:], xt[:], tmp[:], ALU.add, r=[xt, tmp], w=[xt])
        s.dma('sp', dst_ap, xt[:], r=[xt], w=[dst_buf])

    def phase_outproj(self, l):
        s = self.s
        I = self.I
        wsrc = I['even_w_out'][0] if l == 0 else I['odd_w_out'][0]
        xsrc = I['xin'] if l == 0 else self.S['XR']
        tiles = range(NT) if l == 0 else range(2, NT)
        with s.phase('outp%d' % l):
            pre = self.pre.get('outp%d' % l)
            if pre is not None and pre['loaded']:
                w = pre['w']
            else:
                w = s.sb([128, 8, D], BF16, 'wout')
                self.load_w_bf16(w, wsrc, D, D)
            res = self.norm_res(need_hb=False)
            G = [self.load_mod_bc(l, 2, sg, 'G2') for sg in range(2)]
            res['junk2'] = Rot([res['junk'], s.sb([128, D], F32, 'junk2')])
            res['tmp2'] = Rot([res['tmp'], s.sb([128, D], F32, 'tmp2')])
            pre = self.pre.get('ffn%d' % l)
            if pre is not None:
                self.load_w_bf16(pre['wg'], I['ffn_w_gate'][l], D, DFF)
                self.load_w_bf16(pre['wu'], I['ffn_w_up'][l], D, DFF)
                self.load_w_bf16(pre['wd'], I['ffn_w_down'][l], DFF, D)
                pre['loaded'] = True
            xts = Rot([s.sb([128, D], F32, 'xt') for _ in range(5)])
            ots = Rot([s.sb([128, D], BF16, 'ot') for _ in range(3)])
            oTs = Rot([s.sb([128, 8, 128], BF16, 'oT') for _ in range(2)])
            ys = Rot([s.ps([128, D], F32, 'y') for _ in range(3)])
            tiles = list(tiles)
            ctx = {}

            def stA(t):
                xt = xts.next()
                ot = ots.next()
                s.dma('sp', xt[:], xsrc[t * 128:(t + 1) * 128, :], r=[self.db['XR'][t]] if l else [], w=[xt])
                s.dma('sp', ot[:], self.S['O'][t * 128:(t + 1) * 128, :], r=[self.db['O'][t]], w=[ot])
                ctx[t] = [xt, ot]

            def stB(t):
                xt, ot = ctx[t]
                trp = res['trp'].next()
                for kc in range(8):
                    s.tr(trp[:, kc, :], ot[:, kc * 128:(kc + 1) * 128], res['identb'][:], r=[ot, res['identb']], w=[trp])
                oT = oTs.next()
                s.copy('act', oT[:], trp[:], r=[trp], w=[oT])
                ctx[t] = [xt, oT]

            def stC(t):
                xt, oT = ctx[t]
                y = ys.next()
                for hf in range(2):
                    for kc in range(8):
                        s.mm(y[:, hf * 512:(hf + 1) * 512], oT[:, kc, :], w[:, kc, hf * 512:(hf + 1) * 512],
                             kc == 0, kc == 7, r=[oT, w], w=[y])
                ctx[t] = [xt, y]

            def stD(t):
                xt, y = ctx.pop(t)
                seg = 1 if t < 2 else 0
                self.post_norm_res(y, xt, G[seg], res, self.S['XR'][t * 128:(t + 1) * 128, :], self.db['XR'][t], add_eng='dve')
            n = len(tiles)
            rA = [s.record(stA, t) for t in tiles]
            rB = [s.record(stB, t) for t in tiles]
            rC = [s.record(stC, t) for t in tiles]
            rD = [s.record(stD, t) for t in tiles]
            s.replay(rA[0])
            s.replay(rB[0])
            s.replay(rA[1])
            pend = []
            for i in range(n + 1):
                if i < n:
                    s.replay(rC[i])
                if i + 1 < n:
                    s.replay(rB[i + 1])
                if i - 1 >= 0:
                    pend.append(rD[i - 1])
                if len(pend) == 2 or (i == n and pend):
                    s.replay_zip(pend[0], pend[1] if len(pend) > 1 else [])
                    pend = []
                if i + 2 < n:
                    s.replay(rA[i + 2])

    def phase_ffn(self, l):
        s = self.s
        I = self.I
        last = l == 1
        groups = [(t, t + 1) for t in range(0 if not last else 2, NT, 2)]
        with s.phase('ffn%d' % l):
            pre = self.pre.get('ffn%d' % l)
            if pre is not None and pre['loaded']:
                wg, wu, wd = pre['wg'], pre['wu'], pre['wd']
            else:
                wg = s.sb([128, 8, DFF], BF16, 'wg')
                wu = s.sb([128, 8, DFF], BF16, 'wu')
                wd = s.sb([128, NFF, D], BF16, 'wd')
                self.load_w_bf16(wg, I['ffn_w_gate'][l], D, DFF)
                self.load_w_bf16(wu, I['ffn_w_up'][l], D, DFF)
                self.load_w_bf16(wd, I['ffn_w_down'][l], DFF, D)
            res = self.norm_res(ntrp=1)
            A = s.sb([128, D], F32, 'A4')
            B = s.sb([128, D], F32, 'B4')
            Gs = [self.load_mod_bc(l, 5, sg, 'G5') if (sg == 0 or not last) else None for sg in range(2)]
            xts = Rot([s.sb([128, D], F32, 'xt') for _ in range(6)])
            hTs = Rot([s.sb([128, 8, 256], BF16, 'hT') for _ in range(2)])
            aTs = Rot([s.sb([128, 256], BF16, 'aT') for _ in range(4)])
            sgs = Rot([s.sb([128, 256], F32, 'sg') for _ in range(2)])
            pgs = Rot([s.ps([128, 512], F32, 'pg') for _ in range(3)])
            ys = [s.ps([128, D], F32, 'y') for _ in range(2)]
            ysb = [s.sb([128, D], F32, 'ysb') for _ in range(2)]
            ctx = {}
            ng = len(groups)

            def stNorm(gi):
                grp = groups[gi]
                seg = 1 if grp[0] < 2 else 0
                if seg != st8['seg']:
                    st8['seg'] = seg
                    for (tile_, v) in ((A, 3), (B, 4)):
                        s.dma('sp', tile_[:], bcast_row(self.S['MODS'][l, v, seg:seg + 1, :]), r=[self.db_mods], w=[tile_])
                xt_g, hb_g = [], []
                for i, t in enumerate(grp):
                    xt = xts.next()
                    xt_g.append(xt)
                    s.dma('sp', xt[:], self.S['XR'][t * 128:(t + 1) * 128, :], r=[self.db['XR'][t]], w=[xt])
                    hb_g.append(self.norm_mod_T(xt, A, B, None, None, res))
                ctx[gi] = dict(xt=xt_g, hb=hb_g)

            def stTr(gi):
                hT = hTs.next()
                for i in range(2):
                    self.transpose_T(ctx[gi]['hb'][i], hT, slice(i * 128, (i + 1) * 128), res)
                ctx[gi]['hT'] = hT

            def stPost(gi):
                grp = groups[gi]
                for i, t in enumerate(grp):
                    if last:
                        dst, dbuf = self.out[(t - 2) * 128:(t - 1) * 128, :], self.db['OUT'][t]
                    else:
                        dst, dbuf = self.S['XR'][t * 128:(t + 1) * 128, :], self.db['XR'][t]
                    self.post_norm_res(ysb[i], ctx[gi]['xt'][i], Gs[1 if grp[0] < 2 else 0], res, dst, dbuf)
                ctx.pop(gi)

            def chunk_gu(gi, c):
                hT = ctx[gi]['hT']
                pg = pgs.next()
                for kc in range(8):
                    s.mm(pg[:, 0:256], wg[:, kc, c * 128:(c + 1) * 128], hT[:, kc, :], kc == 0, kc == 7, r=[wg, hT], w=[pg])
                for kc in range(8):
                    s.mm(pg[:, 256:512], wu[:, kc, c * 128:(c + 1) * 128], hT[:, kc, :], kc == 0, kc == 7, r=[wu, hT], w=[pg])
                return pg

            def chunk_act(pg):
                sg = sgs.next()
                aT = aTs.next()
                s.act(sg[:], pg[:, 0:256], AF.Silu, r=[pg], w=[sg])
                s.tt('dve', aT[:], sg[:], pg[:, 256:512], ALU.mult, r=[sg, pg], w=[aT])
                return aT

            def chunk_down(pc, aT):
                for i in range(2):
                    for hf in range(2):
                        s.mm(ys[i][:, hf * 512:(hf + 1) * 512], aT[:, i * 128:(i + 1) * 128],
                             wd[:, pc, hf * 512:(hf + 1) * 512], pc == 0, pc == NFF - 1, r=[aT, wd], w=[ys[i]])
            st8 = {'seg': None}
            rN = [s.record(stNorm, gi) for gi in range(ng)]
            s.replay(rN[0])
            stTr(0)
            for gi in range(ng):
                pend = []
                for c in range(NFF + 2):
                    if c < NFF:
                        pg = chunk_gu(gi, c)
                    if pend and (len(pend) == 2 or c >= NFF):
                        chunk_down(*pend.pop(0))
                    if c < NFF:
                        pend.append((c, chunk_act(pg)))
                    if c == 1 and gi >= 1:
                        stPost(gi - 1)
                    if c == 6 and gi + 1 < ng:
                        s.replay(rN[gi + 1])
                    if c == 17 and gi + 1 < ng:
                        stTr(gi + 1)
                for i in range(2):
                    s.copy('act', ysb[i][:], ys[i][:], r=[ys[i]], w=[ysb[i]])
            stPost(ng - 1)

    def phase_scan(self, l):
        s = self.s
        I = self.I
        gla = l == 0
        H = 4 if gla else 8
        NB = 2 if gla else 8
        F = NB * 128
        HV = H * 128
        NG = 1 if gla else 2
        qscale = 0.125 if gla else 1.0
        usc = -1.0 / 16.0 if gla else 1.0
        P16 = self.S['P0'] if gla else self.S['P1A']
        p16b = self.db['P0'] if gla else self.db['P1A']
        with s.phase('scan%d' % l):
            identb = self.load_ident_bf16()
            cst = self.load_consts(['ufwd', 'ubwd', 'mfwd', 'mbwd'])
            U = [s.sb([128, 128], F32, 'U') for _ in range(2)]
            MK = [s.sb([128, 128], F32, 'MK') for _ in range(2)]
            for d, (un, mn) in enumerate((('ufwd', 'mfwd'), ('ubwd', 'mbwd'))):
                s.ts('dve', U[d][:], cst[un][:], usc, None, ALU.mult, None, r=[cst[un]], w=[U[d]])
                s.copy('dve', MK[d][:], cst[mn][:], r=[cst[mn]], w=[MK[d]])
            gain = s.sb([128, 128], F32, 'gain')
            s.dma('sp', gain[:], bcast_row((I['gla_out_norm'] if gla else I['hgrn_out_norm'])[0:1, :]), w=[gain])
            eps = s.sb([128, 1], F32, 'eps')
            s.op('dve', lambda e: e.memset(eps[:], EPS), w=[eps])
            ones1 = s.sb([128, 1], F32, 'ones1')
            s.op('dve', lambda e: e.memset(ones1[:], 1.0), w=[ones1])
            if not gla:
                self.prefetch('outp1', I['odd_w_out'][0], D, D)
            if gla:
                WG = s.sb([33, 512], BF16, 'WG')
                s.op('dve', lambda e: e.memset(WG[:], 0.0), w=[WG])
                s.dma('pool', WG[0:16, 0:256], I['gla_w_gate'][0, 0], w=[WG])
                s.dma('pool', WG[16:32, 256:512], I['gla_w_gate'][0, 1], w=[WG])
                s.dma('pool', WG[32:33, :], I['gla_b_gate'][0:1].rearrange("a d f -> a (d f)"), w=[WG])
                zT = s.sb([33, 128], BF16, 'zT')
                s.op('dve', lambda e: e.memset(zT[:], 1.0), w=[zT])
            if gla:
                GT = [s.ps([128, 4, 128], F32, 'GT') for _ in range(2)]
                ATps = Rot([s.ps([128, 4, 128], F32, 'ATp')])
            else:
                g0_ = s.ps([128, 4, 128], F32, 'GT')
                GT = [g0_, g0_]
                ATps = Rot([s.ps([128, 4, 128], F32, 'ATp') for _ in range(2)])
            TR = Rot([s.ps([128, 8, 128], BF16, 'TR') for _ in range(2)])
            OP = s.ps([128, 4, 128], F32, 'OP')
            DS = Rot([s.ps([128, 4, 128], F32, 'DS') for _ in range(2)])
            raw16 = Rot([s.sb([128, 1024 if gla else 2048], BF16, 'raw16') for _ in range(3)])
            if gla:
                gzs = Rot([s.sb([128, 32], BF16, 'gz') for _ in range(3)])
                ex = s.sb([128, 256], F32, 'ex')
            else:
                f32s = Rot([s.sb([128, D], F32, 'f32') for _ in range(2)])
                tt_ = s.sb([128, D], F32, 'tt')
                qs = Rot([s.sb([128, D], BF16, 'qs') for _ in range(2)])
                ks = Rot([s.sb([128, D], BF16, 'ks') for _ in range(3)])
            gsrc = Rot([s.sb([128, F], F32, 'gsrc') for _ in range(3)])
            Ep = Rot([s.sb([128, NB, 128], F32, 'Ep') for _ in range(2)])
            Em = Rot([s.sb([128, NB, 128], F32, 'Em') for _ in range(2)])
            QA = Rot([s.sb([128, H, 128], BF16, 'QA') for _ in range(2)])
            QB = Rot([s.sb([128, H, 128], BF16, 'QB') for _ in range(2)])
            for q_ in QA.items + QB.items:
                s.op('pool', lambda e, q_=q_: e.memset(q_[:], 0.0), w=[q_])
            KT = Rot([s.sb([128, NB, 128], BF16, 'KT') for _ in range(2)])
            KTM = Rot([s.sb([128, NB, 128], BF16, 'KTM') for _ in range(2)])
            DEC = Rot([s.sb([128, NB, 2], F32, 'DEC') for _ in range(2)])
            ATm = Rot([s.sb([128, 4, 128], BF16, 'ATm') for _ in range(2)])
            Sf = Rot([s.sb([128, NB, 128], F32, 'Sf') for _ in range(3)])
            Sb = Rot([s.sb([128, NB, 128], BF16, 'Sb') for _ in range(4)])
            stmp = Rot([s.sb([128, 4, 128], F32, 'stmp') for _ in range(2)])
            ofs = Rot([s.sb([128, HV], F32, 'of') for _ in range(2)])
            osum = s.sb([128, H, 128], F32, 'osum')
            junk = s.sb([128, H, 128], F32, 'junk')
            ogs = Rot([s.sb([128, HV], BF16, 'og') for _ in range(2)])
            sgt = s.sb([128, HV], F32, 'sgt')
            ss = Rot([s.sb([128, H], F32, 'ss') for _ in range(2)])
            rstd = Rot([s.sb([128, H], F32, 'rstd') for _ in range(2)])
            ob = Rot([s.sb([128, HV], BF16, 'ob') for _ in range(2)])

            import os as _os
            for d in range(int(_os.environ.get('SCAN_DIRS', '2'))):
                fwd = d == 0
                order = list(range(NT)) if fwd else [1, 0] + list(range(NT - 1, 1, -1))
                first = 0 if fwd else 1
                S_prev = Sf.next()
                Sb_prev = Sb.next()
                s.op('dve', lambda e, S_prev=S_prev: e.memset(S_prev[:], 0.0), w=[S_prev])
                s.op('pool', lambda e, Sb_prev=Sb_prev: e.memset(Sb_prev[:], 0.0), w=[Sb_prev])
                order = order[:int(_os.environ.get('SCAN_TILES', '99'))]
                ctx = {}
                stt8 = {'S': S_prev, 'Sb': Sb_prev}

                def stA1(t):
                    c_ = ctx.setdefault(t, {})
                    rows = slice(t * 128, (t + 1) * 128)
                    r16 = raw16.next()
                    if gla:
                        s.dma('sp', r16[:], P16[rows, 0:1024], r=[p16b[t]], w=[r16])
                        gz = gzs.next()
                        s.dma('sp', gz[:], P16[rows, 1536:1568], r=[p16b[t]], w=[gz])
                        c_.update(q_tm=r16, k_tm=r16, v_tm=r16, q0=0, k0=256, v0=512)
                    else:
                        s.dma('sp', r16[:], P16[rows, 0:2048], r=[p16b[t]], w=[r16])
                        f32 = f32s.next()
                        s.dma('sp', f32[:], self.S['P1B'][rows, d * 1024:(d + 1) * 1024], r=[self.db['P1B'][t]], w=[f32])
                        c_.update(v_tm=r16, v0=1024)
                    gs = gsrc.next()
                    c_['gs'] = gs
                    if gla:
                        trp = TR.next()
                        s.tr(trp[0:32, 0, :], gz[:], identb[:], r=[gz, identb], w=[trp])
                        s.copy('act', zT[0:32, :], trp[0:32, 0, :], r=[trp], w=[zT])
                        lg = GT[1]
                        lgv = lg[:].rearrange("p a b -> p (a b)")[:, 0:256]
                        s.mm(lgv, zT[:], WG[:, d * 256:(d + 1) * 256], True, True, r=[zT, WG], w=[lg])
                        s.act(ex[:], lgv, AF.Exp, r=[lg], w=[ex], scale=-1.0)
                        s.act(gs[:], ex[:], AF.Ln, r=[ex], w=[gs], bias=1.0)
                    else:
                        s.act(gs[:], f32[:], AF.Ln, r=[f32], w=[gs])
                        k_tm = ks.next()
                        s.act(k_tm[:], f32[:], AF.Identity, r=[f32, ones1], w=[k_tm], bias=ones1[:], scale=-1.0)
                        c_.update(q_tm=r16, k_tm=k_tm, q0=0, k0=0)

                def stA2(t):
                    c_ = ctx[t]
                    gs = c_['gs']
                    ep, em, dec = Ep.next(), Em.next(), DEC.next()
                    for g in range((NB + 3) // 4):
                        nb_ = min(4, NB - 4 * g)
                        for fb in range(4 * g, 4 * g + nb_):
                            s.mm(GT[g][:, fb % 4, :], gs[:, fb * 128:(fb + 1) * 128], U[d][:], True, True,
                                 r=[gs, U[d]], w=[GT[g]])
                        s.act(ep[:, 4 * g:4 * g + nb_, :], GT[g][:, 0:nb_, :], AF.Exp, r=[GT[g]], w=[ep])
                        s.act(em[:, 4 * g:4 * g + nb_, :], GT[g][:, 0:nb_, :], AF.Exp, r=[GT[g]], w=[em], scale=-1.0)
                    c_off = 63 if fwd else 0
                    s.copy('act', dec[:], ep[:, :, c_off::64], r=[ep], w=[dec])
                    c_.update(ep=ep, em=em, dec=dec)

                def stA3(t):
                    c_ = ctx[t]
                    ep, em, q_tm, k_tm, q0, k0 = c_['ep'], c_['em'], c_['q_tm'], c_['k_tm'], c_['q0'], c_['k0']
                    qa, qb, kt, ktm = QA.next(), QB.next(), KT.next(), KTM.next()
                    trq = TR.next()
                    for fb in range(NB):
                        s.tr(trq[:, fb, :], q_tm[:, q0 + fb * 128:q0 + (fb + 1) * 128], identb[:], r=[q_tm, identb], w=[trq])
                    if gla:
                        for (pr_, par) in ((slice(0, 64), 0), (slice(64, 128), 1)):
                            s.stt(qa[pr_, par::2, 0:64], trq[pr_, 0:NB, 0:64], qscale, ep[pr_, :, 0:64], ALU.mult, ALU.mult, r=[trq, ep], w=[qa])
                            s.stt(qb[pr_, par::2, 64:128], trq[pr_, 0:NB, 64:128], qscale, ep[pr_, :, 64:128], ALU.mult, ALU.mult, r=[trq, ep], w=[qb])
                    else:
                        s.stt(qa[:, :, 0:64], trq[:, 0:NB, 0:64], qscale, ep[:, :, 0:64], ALU.mult, ALU.mult, r=[trq, ep], w=[qa])
                        s.stt(qb[:, :, 64:128], trq[:, 0:NB, 64:128], qscale, ep[:, :, 64:128], ALU.mult, ALU.mult, r=[trq, ep], w=[qb])
                    trk = TR.next()
                    for fb in range(NB):
                        s.tr(trk[:, fb, :], k_tm[:, k0 + fb * 128:k0 + (fb + 1) * 128], identb[:], r=[k_tm, identb], w=[trk])
                    s.tt('dve', kt[:], trk[:, 0:NB, :], em[:], ALU.mult, r=[trk, em], w=[kt])
                    trm = TR.next()
                    for fb in range(NB):
                        s.tr(trm[:, fb, :], kt[:, fb, :], identb[:], r=[kt, identb], w=[trm])
                    s.copy('act', ktm[:], trm[:, 0:NB, :], r=[trm], w=[ktm])
                    c_.update(qa=qa, qb=qb, kt=kt, ktm=ktm)

                def state_step(t, which):
                    c_ = ctx[t]
                    ktm, v_tm, v0, dec = c_['ktm'], c_['v_tm'], c_['v0'], c_['dec']
                    c = first if which == 0 else 1 - first
                    S_in = stt8['S']
                    S_out, Sb_out = Sf.next(), Sb.next()
                    if which == 0:
                        c_['Sb_prev'] = stt8['Sb']
                        c_['Sb_mid'] = Sb_out
                    crow = slice(c * 64, (c + 1) * 64)
                    grp_ = []
                    for g in range(NG):
                        ds = DS.next()
                        st_ = stmp.next()
                        if gla:
                            dsv = ds[:].rearrange("p a b -> p (a b)")
                            for fb in range(2):
                                s.mm(dsv[:, fb * 256:(fb + 1) * 256], ktm[crow, fb, :],
                                     v_tm[crow, v0 + fb * 256:v0 + (fb + 1) * 256], True, True, r=[ktm, v_tm], w=[ds])
                            dsp = ds[:].rearrange("p (f a) b -> p f a b", a=2)
                            s.tt('dve', st_[0:64, 0:2, :], dsp[0:64, :, 0, :], S_in[0:64, :, :], ALU.add, r=[ds, S_in], w=[st_])
                            s.tt('dve', st_[64:128, 0:2, :], dsp[64:128, :, 1, :], S_in[64:128, :, :], ALU.add, r=[ds, S_in], w=[st_])
                            nb_, b0_ = 2, 0
                        else:
                            for hh in range(4):
                                h = 4 * g + hh
                                s.mm(ds[:, hh, :], ktm[crow, h, :], v_tm[crow, v0 + h * 128:v0 + (h + 1) * 128],
                                     True, True, r=[ktm, v_tm], w=[ds])
                            s.tt('dve', st_[:], ds[:], S_in[:, 4 * g:4 * g + 4, :], ALU.add, r=[ds, S_in], w=[st_])
                            nb_, b0_ = 4, 4 * g
                        grp_.append((st_, nb_, b0_))
                    for (st_, nb_, b0_) in grp_:
                        dcb = dec[:, b0_:b0_ + nb_, c:c + 1].broadcast_to([128, nb_, 128])
                        s.tt('dve', S_out[:, b0_:b0_ + nb_, :], st_[:, 0:nb_, :], dcb, ALU.mult, r=[st_, dec], w=[S_out])
                    for (st_, nb_, b0_) in grp_:
                        s.copy('dve', Sb_out[:, b0_:b0_ + nb_, :], S_out[:, b0_:b0_ + nb_, :], r=[S_out], w=[Sb_out])
                    stt8['S'], stt8['Sb'] = S_out, Sb_out

                def stB1(t):
                    state_step(t, 0)

                def stB2(t):
                    state_step(t, 1)

                def stB3(t):
                    c_ = ctx.pop(t)
                    need_o = not (l == 1 and t < 2)
                    if not need_o:
                        return
                    rows = slice(t * 128, (t + 1) * 128)
                    qa, qb, kt, v_tm, v0 = c_['qa'], c_['qb'], c_['kt'], c_['v_tm'], c_['v0']
                    Sb_prev, Sb_mid = c_['Sb_prev'], c_['Sb_mid']
                    of = ofs.next()
                    if not fwd:
                        s.dma('sp', of[:], self.S['OF'][rows, 0:HV], r=[self.db['OF'][t]], w=[of])
                    SA, SB_ = (Sb_prev, Sb_mid) if fwd else (Sb_mid, Sb_prev)
                    atms = []
                    for g in range(NG):
                        ATp = ATps.next()
                        for hh in range(4):
                            h = 4 * g + hh
                            fb = h // 2 if gla else h
                            s.mm(ATp[:, hh, 0:64], kt[:, fb, :], qa[:, h, 0:64], True, True, r=[kt, qa], w=[ATp])
                            s.mm(ATp[:, hh, 64:128], kt[:, fb, :], qb[:, h, 64:128], True, True, r=[kt, qb], w=[ATp])
                        atm = ATm.next()
                        mkb = MK[d][:].unsqueeze(1).broadcast_to([128, 4, 128])
                        s.tt('dve', atm[:], ATp[:], mkb, ALU.mult, r=[ATp, MK[d]], w=[atm])
                        atms.append(atm)
                    for g in range(NG):
                        atm = atms[g]
                        for hh in range(4):
                            h = 4 * g + hh
                            fb = h // 2 if gla else h
                            s.mm(OP[:, hh, :], atm[:, hh, :], v_tm[:, v0 + h * 128:v0 + (h + 1) * 128], True, False,
                                 r=[atm, v_tm], w=[OP])
                            s.mm(OP[:, hh, :], qa[:, h, :], SA[:, fb, :], False, False, r=[qa, SA], w=[OP])
                            s.mm(OP[:, hh, :], qb[:, h, :], SB_[:, fb, :], False, True, r=[qb, SB_], w=[OP])
                        opv = OP[:].rearrange("p a b -> p (a b)")
                        if fwd:
                            s.copy('act', of[:, g * 512:(g + 1) * 512], opv, r=[OP], w=[of])
                        else:
                            s.tt('dve', osum[:, 4 * g:4 * g + 4, :], OP[:], of[:, g * 512:(g + 1) * 512].rearrange("p (a b) -> p a b", b=128),
                                 ALU.add, r=[OP, of], w=[osum])
                    if fwd:
                        s.dma('sp', self.S['OF'][rows, 0:HV], of[:], r=[of], w=[self.db['OF'][t]])
                    else:
                        og = ogs.next()
                        ogc = 1024 if gla else 2048
                        s.dma('sp', og[:], P16[rows, ogc:ogc + HV], r=[p16b[t]], w=[og])
                        ss_, rs_ = ss.next(), rstd.next()
                        s.tt('pool', junk[:], osum[:], osum[:], ALU.mult, r=[osum], w=[junk])
                        s.op('dve', lambda e, ss_=ss_: e.tensor_reduce(ss_[:], junk[:], AX.X, ALU.add), r=[junk], w=[ss_])
                        s.act(ss_[:], ss_[:], AF.Sqrt, r=[ss_, eps], w=[ss_], bias=eps[:], scale=1.0 / 128)
                        s.op('dve', lambda e, ss_=ss_, rs_=rs_: e.reciprocal(rs_[:], ss_[:]), r=[ss_], w=[rs_])
                        s.tt('dve', osum[:], osum[:], rs_[:].unsqueeze(2).broadcast_to([128, H, 128]), ALU.mult, r=[osum, rs_], w=[osum])
                        s.tt('pool', osum[:], osum[:], gain[:].unsqueeze(1).broadcast_to([128, H, 128]), ALU.mult, r=[osum, gain], w=[osum])
                        o_ = ob.next()
                        s.tt('dve', o_[:], osum[:].rearrange("p a b -> p (a b)"), og[:], ALU.mult, r=[osum, og], w=[o_])
                        s.dma('sp', self.S['O'][rows, 0:HV], o_[:], r=[o_], w=[self.db['O'][t]])
                stages = [stA1, stA2, stA3, stB1, stB2, stB3]
                recs = [[] for _ in stages]
                for t in order:
                    for k, f in enumerate(stages):
                        recs[k].append(s.record(f, t))
                n_ = len(order)
                s.replay(recs[0][0])
                s.replay(recs[0][1])
                s.replay(recs[1][0])
                s.replay(recs[2][0])
                for i in range(n_):
                    if i + 2 < n_:
                        s.replay(recs[0][i + 2])
                    s.replay(recs[3][i])
                    if i + 1 < n_:
                        s.replay(recs[1][i + 1])
                    s.replay(recs[4][i])
                    if i + 1 < n_:
                        s.replay(recs[2][i + 1])
                    s.replay(recs[5][i])

    def phase_att(self):
        s = self.s
        I = self.I
        P0 = self.S['P0']
        QC, KC, VC = 1568, 2080, 2208
        with s.phase('att'):
            identb = self.load_ident_bf16()
            identf = s.sb([128, 128], F32, 'identf')
            s.dma('sp', identf[:], I['ident'][:, :], w=[identf])
            eps = s.sb([128, 1], F32, 'eps')
            s.op('dve', lambda e: e.memset(eps[:], EPS), w=[eps])
            GQK = s.sb([128, 10, 64], F32, 'GQK')
            for h in range(10):
                src = I['att_q_norm'] if h < 8 else I['att_k_norm']
                s.dma('sp', GQK[:, h, :], bcast_row(src[0:1, :]), w=[GQK])
            s.ts('dve', GQK[:, 0:8, :], GQK[:, 0:8, :], 0.125, None, ALU.mult, None, r=[GQK], w=[GQK])
            QT = s.sb([128, 4, NTOK], BF16, 'QT')
            KT = [s.sb([128, NTOK], BF16, 'KT%d' % kv) for kv in range(2)]
            for kv in range(2):
                s.op('pool', lambda e, kv=kv: e.memset(KT[kv][:], 0.0), w=[KT[kv]])
            VA = s.sb([128, NT, 2, 66], BF16, 'VA')
            s.op('pool', lambda e: e.memset(VA[:], 1.0), w=[VA])
            self.prefetch('outp0', I['even_w_out'][0], D, D)
            raws = Rot([s.sb([128, 10, 64], BF16, 'raw') for _ in range(2)])
            vraws = Rot([s.sb([128, 2, 64], BF16, 'vraw') for _ in range(2)])
            coss = Rot([s.sb([128, 32], F32, 'cos') for _ in range(2)])
            sins = Rot([s.sb([128, 32], F32, 'sin') for _ in range(2)])
            junk = s.sb([128, 10, 64], F32, 'junk')
            xn = s.sb([128, 10, 32, 2], F32, 'xn')
            tA = s.sb([128, 10, 32], F32, 'tA')
            tB = s.sb([128, 10, 32], F32, 'tB')
            tC = s.sb([128, 10, 32], F32, 'tC')
            tD = s.sb([128, 10, 32], F32, 'tD')
            ss = Rot([s.sb([128, 10], F32, 'ss') for _ in range(2)])
            rstd = Rot([s.sb([128, 10], F32, 'rstd') for _ in range(2)])
            rqs = Rot([s.sb([128, 10, 32, 2], BF16, 'rq') for _ in range(2)])
            trps = Rot([s.ps([128, 8, 128], BF16, 'trp') for _ in range(1)])
            import os as _os
            _stop = int(_os.environ.get('ATT_STOP', '9'))
            _pt = int(_os.environ.get('ATT_PTILES', '99'))
            _qt = int(_os.environ.get('ATT_QTILES', '99'))
            for t in range(min(NT, _pt)):
                rows = slice(t * 128, (t + 1) * 128)
                raw, vraw = raws.next(), vraws.next()
                for j in range(2):
                    s.dma('sp', raw[:, j:8:2, :], P0[rows, QC + j * 256:QC + (j + 1) * 256].rearrange("p (i d) -> p i d", d=64),
                          r=[self.db['P0'][t]], w=[raw])
                s.dma('sp', raw[:, 8:10, :].rearrange("p a b -> p (a b)"), P0[rows, KC:VC], r=[self.db['P0'][t]], w=[raw])
                s.dma('sp', vraw[:].rearrange("p a b -> p (a b)"), P0[rows, VC:VC + 128], r=[self.db['P0'][t]], w=[vraw])
                s.copy('pool', VA[:, t, :, 0:64], vraw[:], r=[vraw], w=[VA])
                ss_, rs_ = ss.next(), rstd.next()
                s.tt('pool', junk[:], raw[:], raw[:], ALU.mult, r=[raw], w=[junk])
                s.op('dve', lambda e, ss_=ss_: e.tensor_reduce(ss_[:], junk[:], AX.X, ALU.add), r=[junk], w=[ss_])
                s.act(ss_[:], ss_[:], AF.Sqrt, r=[ss_, eps], w=[ss_], bias=eps[:], scale=1.0 / 64)
                s.op('dve', lambda e, ss_=ss_, rs_=rs_: e.reciprocal(rs_[:], ss_[:]), r=[ss_], w=[rs_])
                xnv = xn[:].rearrange("p a b c -> p a (b c)")
                s.tt('dve', xnv, raw[:], rs_[:].unsqueeze(2).broadcast_to([128, 10, 64]), ALU.mult, r=[raw, rs_], w=[xn])
                rq = rqs.next()
                if t < 2:
                    s.tt('dve', rq[:].rearrange("p a b c -> p a (b c)"), xnv, GQK[:], ALU.mult, r=[xn, GQK], w=[rq])
                else:
                    s.tt('pool', xnv, xnv, GQK[:], ALU.mult, r=[xn, GQK], w=[xn])
                    cs, sn = coss.next(), sins.next()
                    s.dma('sp', cs[:], I['cos'][(t - 2) * 128:(t - 1) * 128, :], w=[cs])
                    s.dma('sp', sn[:], I['sin'][(t - 2) * 128:(t - 1) * 128, :], w=[sn])
                    cb = cs[:].unsqueeze(1).broadcast_to([128, 10, 32])
                    sb_ = sn[:].unsqueeze(1).broadcast_to([128, 10, 32])
                    x0, x1 = xn[:, :, :, 0], xn[:, :, :, 1]
                    s.tt('dve', tA[:], x0, cb, ALU.mult, r=[xn, cs], w=[tA])
                    s.tt('pool', tB[:], x1, sb_, ALU.mult, r=[xn, sn], w=[tB])
                    s.tt('dve', rq[:, :, :, 0], tA[:], tB[:], ALU.subtract, r=[tA, tB], w=[rq])
                    s.tt('pool', tC[:], x0, sb_, ALU.mult, r=[xn, sn], w=[tC])
                    s.tt('dve', tD[:], x1, cb, ALU.mult, r=[xn, cs], w=[tD])
                    s.tt('dve', rq[:, :, :, 1], tC[:], tD[:], ALU.add, r=[tC, tD], w=[rq])
                if _stop < 1:
                    continue
                rqv = rq[:].rearrange("p a b c -> p a (b c)")
                trp = trps.next()
                for i in range(4):
                    s.tr(trp[:, i, :], rqv[:, 2 * i:2 * i + 2, :], identb[:], r=[rq, identb], w=[trp])
                s.tr(trp[:, 4, :], rqv[:, 8:10, :], identb[:], r=[rq, identb], w=[trp])
                s.copy('act', QT[:, :, rows], trp[:, 0:4, :], r=[trp], w=[QT])
                s.copy('dve', KT[0][0:64, rows], trp[0:64, 4, :], r=[trp], w=[KT[0]])
                s.copy('dve', KT[1][64:128, rows], trp[64:128, 4, :], r=[trp], w=[KT[1]])
            SPs = Rot([s.ps([128, 2, 512], F32, 'SP') for _ in range(2)])
            ACCs = Rot([s.ps([66, 512], F32, 'ACC') for _ in range(2)])
            TP = s.ps([128, 4, 128], F32, 'TP')
            PTs = Rot([s.sb([128, 2, 512], BF16, 'PT') for _ in range(3)])
            accs = Rot([s.sb([66, 512], F32, 'accs') for _ in range(3)])
            rcp = Rot([s.sb([128, 4], F32, 'rcp') for _ in range(2)])
            obs = Rot([s.sb([128, 512], BF16, 'ob') for _ in range(3)])
            items = []
            for qt in range(min(NT, _qt) if _stop >= 2 else 0):
                keys = [0, 1] if qt < 2 else list(range(NT))
                for kv in range(2):
                    for n in range(0, len(keys), 2):
                        items.append((qt, kv, keys[n:n + 2], n == 0, n + 2 >= len(keys)))
            ctx = {}
            fin = {}
            st8 = {'ob': None, 'acc': None}

            def stQK(i):
                qt, kv, kts, first, last = items[i]
                rows = slice(qt * 128, (qt + 1) * 128)
                sp = SPs.next()
                for j, kt in enumerate(kts):
                    s.mm(sp[:, j, :], KT[kv][:, kt * 128:(kt + 1) * 128], QT[:, :, rows], True, True, r=[KT[kv], QT], w=[sp])
                ctx[i] = sp

            def stEXP(i):
                sp = ctx[i]
                pt = PTs.next()
                s.act(pt[:], sp[:], AF.Exp, r=[sp], w=[pt])
                ctx[i] = pt

            def stPV(i):
                qt, kv, kts, first, last = items[i]
                rows = slice(qt * 128, (qt + 1) * 128)
                pt = ctx.pop(i)
                if first:
                    st8['acc'] = ACCs.next()
                    if kv == 0:
                        st8['ob'] = obs.next()
                acc, ob = st8['acc'], st8['ob']
                for j, kt in enumerate(kts):
                    s.mm(acc[:], VA[:, kt, kv, :], pt[:, j, :], first and j == 0, last and j == len(kts) - 1, r=[VA, pt], w=[acc])
                if not last:
                    return
                ac = accs.next()
                s.copy('dve', ac[:], acc[:], r=[acc], w=[ac])
                fin[i] = (ac, ob)

            def stFIN(i):
                if i not in fin:
                    return
                qt, kv, kts, first, last = items[i]
                rows = slice(qt * 128, (qt + 1) * 128)
                ac, ob = fin.pop(i)
                for h in range(4):
                    s.tr(TP[:, h, 0:66], ac[:, h * 128:(h + 1) * 128], identf[0:66, 0:66], r=[ac, identf], w=[TP])
                rc = rcp.next()
                s.op('dve', lambda e, rc=rc: e.reciprocal(rc[:], TP[:, :, 64]), r=[TP], w=[rc])
                s.tt('dve', ob[:, kv * 256:(kv + 1) * 256].rearrange("p (a b) -> p a b", b=64), TP[:, :, 0:64],
                     rc[:].unsqueeze(2).broadcast_to([128, 4, 64]), ALU.mult, r=[TP, rc], w=[ob])
                if kv == 1:
                    s.dma('sp', self.S['O'][rows, 512:1024], ob[:], r=[ob], w=[self.db['O'][qt]])
            n_it = len(items)
            rQ = [s.record(stQK, i) for i in range(n_it)]
            rE = [s.record(stEXP, i) for i in range(n_it)]
            rP = [s.record(stPV, i) for i in range(n_it)]
            rF = [s.record(stFIN, i) for i in range(n_it)]
            for step in range(n_it + 5):
                if step < n_it:
                    s.replay(rQ[step])
                if 0 <= step - 1 < n_it:
                    s.replay(rE[step - 1])
                if 0 <= step - 2 < n_it:
                    s.replay(rP[step - 2])
                if 0 <= step - 4 < n_it:
                    s.replay(rF[step - 4])

    def phase_mods(self):
        s = self.s
        I = self.I
        with s.phase('mods'):
            self.prefetch('proj0', I['even_w_in'][0], D, F0)
            cc = s.sb([128, 16], F32, 'cc')
            sc = s.sb([128, 8, 2], F32, 'sc')
            s.dma('sp', cc[:], I['cc'][:, :], w=[cc])
            s.act(sc[:].rearrange("p a b -> p (a b)"), cc[:], AF.Silu, r=[cc], w=[sc])
            wb = Rot([s.sb([128, 8, 512], F32, 'mw') for _ in range(3)])
            pb = Rot([s.ps([2, 512], F32, 'mp') for _ in range(2)])
            mrow = s.sb([2, 6 * D], F32, 'mrow')
            bias = s.sb([2, 6 * D], F32, 'mbias')
            der = s.sb([2, 6, D], F32, 'der')
            gains = {gn: s.sb([2, D], F32, gn) for gn in ['norm_pre_mix', 'norm_post_mix', 'norm_pre_ffn', 'norm_post_ffn']}
            for l in range(2):
                s.dma('sp', bias[:], bcast_row(I['mod_b'][l:l + 1, :], 2), w=[bias])
                for gn in gains:
                    s.dma('sp', gains[gn][:], bcast_row(I[gn][l:l + 1, :], 2), w=[gains[gn]])
                wsrc = I['mod_w'][l].rearrange("(kc p) f -> p kc f", p=128)
                for nb in range(12):
                    w = wb.next()
                    s.dma('sp', w[:], wsrc[:, :, nb * 512:(nb + 1) * 512], w=[w])
                    p = pb.next()
                    for kc in range(8):
                        s.mm(p[:], sc[:, kc, :], w[:, kc, :], kc == 0, kc == 7, r=[sc, w], w=[p])
                    s.tt('dve', mrow[:, nb * 512:(nb + 1) * 512], p[:], bias[:, nb * 512:(nb + 1) * 512],
                         ALU.add, r=[p, bias], w=[mrow])
                def m(i):
                    return mrow[:, i * D:(i + 1) * D]
                s.stt(der[:, 0, :], m(1), 1.0, gains['norm_pre_mix'][:], ALU.add, ALU.mult,
                      r=[mrow, gains['norm_pre_mix']], w=[der])
                s.copy('dve', der[:, 1, :], m(0), r=[mrow], w=[der])
                s.tt('dve', der[:, 2, :], m(2), gains['norm_post_mix'][:], ALU.mult,
                     r=[mrow, gains['norm_post_mix']], w=[der])
                s.stt(der[:, 3, :], m(4), 1.0, gains['norm_pre_ffn'][:], ALU.add, ALU.mult,
                      r=[mrow, gains['norm_pre_ffn']], w=[der])
                s.copy('dve', der[:, 4, :], m(3), r=[mrow], w=[der])
                s.tt('dve', der[:, 5, :], m(5), gains['norm_post_ffn'][:], ALU.mult,
                     r=[mrow, gains['norm_post_ffn']], w=[der])
                s.dma('sp', self.S['MODS'][l].rearrange("v s d -> s v d"), der[:], r=[der], w=[self.db_mods])

    def load_mod_bc(self, l, v, seg, name):
        s = self.s
        t = s.sb([128, D], F32, name)
        s.dma('sp', t[:], bcast_row(self.S['MODS'][l, v, seg:seg + 1, :]), r=[self.db_mods], w=[t])
        return t

    def norm_mod_T(self, xt, A, B, hT, hT_slice, res):
        s = self.s
        junk, ss, rstd, tmp, hb = (res['junk'], res['ss'].next(), res['rstd'].next(), res['tmp'], res['hb'].next())
        s.tt('pool', junk[:], xt[:], xt[:], ALU.mult, r=[xt], w=[junk])
        s.op('dve', lambda e: e.tensor_reduce(ss[:], junk[:], AX.X, ALU.add), r=[junk], w=[ss])
        s.act(ss[:], ss[:], AF.Sqrt, r=[ss, res['eps']], w=[ss], bias=res['eps'][:], scale=1.0 / D)
        s.op('dve', lambda e: e.reciprocal(rstd[:], ss[:]), r=[ss], w=[rstd])
        s.stt(tmp[:], xt[:], rstd[:], A[:], ALU.mult, ALU.mult, r=[xt, rstd, A], w=[tmp])
        s.tt('dve', hb[:], tmp[:], B[:], ALU.add, r=[tmp, B], w=[hb])
        if hT is None:
            return hb
        self.transpose_T(hb, hT, hT_slice, res)

    def transpose_T(self, hb, hT, hT_slice, res):
        s = self.s
        trp, identb = res['trp'].next(), res['identb']
        for kc in range(8):
            s.tr(trp[:, kc, :], hb[:, kc * 128:(kc + 1) * 128], identb[:], r=[hb, identb], w=[trp])
        s.copy('act', hT[:, :, hT_slice], trp[:], r=[trp], w=[hT])

    def norm_res(self, need_hb=True, ntrp=2):
        s = self.s
        eps = s.sb([128, 1], F32, 'eps')
        s.op('dve', lambda e: e.memset(eps[:], EPS), w=[eps])
        return dict(
            junk=s.sb([128, D], F32, 'junk'),
            ss=Rot([s.sb([128, 1], F32, 'ss') for _ in range(2)]),
            rstd=Rot([s.sb([128, 1], F32, 'rstd') for _ in range(2)]),
            tmp=s.sb([128, D], F32, 'tmp'),
            hb=Rot([s.sb([128, D], BF16, 'hb') for _ in range(3)]) if need_hb else None,
            trp=Rot([s.ps([128, 8, 128], BF16, 'trp') for _ in range(ntrp)]),
            identb=self.load_ident_bf16(),
            eps=eps,
        )

    def phase_proj(self, l):
        s = self.s
        I = self.I
        F = F0 if l == 0 else F1
        wsrc = I['even_w_in'][0] if l == 0 else I['odd_w_in'][0]
        xsrc = I['xin'] if l == 0 else self.S['XR']
        xdb = None if l == 0 else self.db['XR']
        with s.phase('proj%d' % l):
            pre = self.pre.get('proj%d' % l)
            wbuf = {}
            if pre is not None and pre['loaded']:
                w = pre['w']
            elif l == 1:
                w = s.sb([128, 8, F], BF16, 'win')
                for (c0, c1) in ((0, 1024), (4096, 5120), (1024, 3072), (3072, 4096)):
                    wbuf[c0] = Buf('w%d' % c0)
                    for kc in range(8):
                        s.dma('pool', w[:, kc, c0:c1], wsrc[kc * 128:(kc + 1) * 128, c0:c1], w=[wbuf[c0]])
            else:
                w = s.sb([128, 8, F], BF16, 'win')
                self.load_w_bf16(w, wsrc, D, F)
            res = self.norm_res()
            A = [self.load_mod_bc(l, 0, sg, 'A1') for sg in range(2)]
            B = [self.load_mod_bc(l, 1, sg, 'B1') for sg in range(2)]
            xts = Rot([s.sb([128, D], F32, 'xt') for _ in range(3)])
            hTs = Rot([s.sb([128, 8, 128], BF16, 'hT') for _ in range(3)])
            pss = Rot([s.ps([128, 512], F32, 'pp') for _ in range(5)])
            if l == 0:
                outs = [(0, F0, 'P0', 0, BF16)]
                fmap = {1024: AF.Silu}
            else:
                outs = [(0, 1024, 'P1A', 0, BF16), (4096, 5120, 'P1A', 2048, BF16), (1024, 3072, 'P1B', 0, F32),
                        (3072, 4096, 'P1A', 1024, BF16)]
                fmap = {0: AF.Silu, 512: AF.Silu, 1024: AF.Sigmoid, 1536: AF.Sigmoid, 2048: AF.Sigmoid, 2560: AF.Sigmoid,
                        4096: AF.Silu, 4608: AF.Silu}
            stg = {}
            for (c0, c1, dn, d0, dt) in outs:
                stg[(c0, dn)] = Rot([s.sb([128, c1 - c0], dt, 'stg') for _ in range(2)])
            ev = Rot(['dve', 'act', 'dve'])
            ctx = {}
            if l == 1:
                LBa = s.sb([128, 2 * D], F32, 'LBa')
                OMLa = s.sb([128, 2 * D], F32, 'OMLa')
                for d_ in range(2):
                    s.dma('sp', OMLa[:, d_ * D:(d_ + 1) * D], bcast_row(I['hgrn_lower_bounds'][d_, 0:1, :]), w=[OMLa])
                    s.dma('sp', LBa[:, d_ * D:(d_ + 1) * D], bcast_row(I['hgrn_lower_bounds'][d_, 1:2, :]), w=[LBa])
                s.tt('dve', LBa[:], LBa[:], OMLa[:], ALU.subtract, r=[LBa, OMLa], w=[LBa])
                s.act(LBa[:], LBa[:], AF.Sigmoid, r=[LBa], w=[LBa])
                s.ts('dve', OMLa[:], LBa[:], -1.0, 1.0, ALU.mult, ALU.add, r=[LBa], w=[OMLa])

            def stA(t):
                seg = 1 if t < 2 else 0
                xt = xts.next()
                s.dma('sp', xt[:], xsrc[t * 128:(t + 1) * 128, :], r=[xdb[t]] if xdb else [], w=[xt])
                ctx[t] = self.norm_mod_T(xt, A[seg], B[seg], None, None, res)

            def stB(t):
                hT = hTs.next()
                self.transpose_T(ctx[t], hT, slice(0, 128), res)
                ctx[t] = hT

            def stC(t):
                hT = ctx.pop(t)
                for (c0, c1, dn, d0, dt) in outs:
                    st = stg[(c0, dn)].next()
                    for n0 in range(c0, c1, 512):
                        n1 = min(n0 + 512, c1)
                        p = pss.next()
                        for kc in range(8):
                            s.mm(p[:, 0:n1 - n0], hT[:, kc, :], w[:, kc, n0:n1], kc == 0, kc == 7, r=[hT, wbuf.get(c0, w)], w=[p])
                        if n0 in fmap:
                            s.act(st[:, n0 - c0:n1 - c0], p[:, 0:n1 - n0], fmap[n0], r=[p], w=[st])
                            if fmap[n0] == AF.Sigmoid:
                                s.tt('dve', st[:, n0 - c0:n1 - c0], st[:, n0 - c0:n1 - c0], OMLa[:, n0 - c0:n1 - c0], ALU.mult,
                                     r=[st, OMLa], w=[st])
                                s.tt('dve', st[:, n0 - c0:n1 - c0], st[:, n0 - c0:n1 - c0], LBa[:, n0 - c0:n1 - c0], ALU.add,
                                     r=[st, LBa], w=[st])
                        else:
                            s.copy(ev.next(), st[:, n0 - c0:n1 - c0], p[:, 0:n1 - n0], r=[p], w=[st])
                    s.dma('sp', self.S[dn][t * 128:(t + 1) * 128, d0:d0 + (c1 - c0)], st[:], r=[st], w=[self.db[dn][t]])
            rA = [s.record(stA, t) for t in range(NT)]
            rB = [s.record(stB, t) for t in range(NT)]
            rC = [s.record(stC, t) for t in range(NT)]
            s.replay(rA[0])
            s.replay(rB[0])
            s.replay(rA[1])
            for t in range(NT):
                s.replay(rC[t])
                if t + 1 < NT:
                    s.replay(rB[t + 1])
                if t + 2 < NT:
                    s.replay(rA[t + 2])


def make_inputs(inputs, b, consts):
    x = np.asarray(inputs['x'])
    ctx = np.asarray(inputs['ctx'])
    m = {}
    m['xin'] = np.ascontiguousarray(np.concatenate([ctx[b], x[b]], axis=0), dtype=np.float32)
    cc = np.stack([np.asarray(inputs['c'])[b], np.asarray(inputs['c_ctx'])], axis=0)
    m['cc'] = np.ascontiguousarray(cc.reshape(2, 8, 128).transpose(2, 1, 0).reshape(128, 16), dtype=np.float32)
    for n in WEIGHT_NAMES:
        m[n] = np.ascontiguousarray(np.asarray(inputs[n]), dtype=np.float32)
    m.update(consts)
    return m


_PROG = None


def kernel(**inputs):
    global _PROG
    if _PROG is None:
        _PROG = Prog()
    consts = host_consts()
    in_maps = [make_inputs(inputs, b, consts) for b in range(8)]
    res = run_bass_kernel_spmd(_PROG.nc, in_maps, core_ids=list(range(8)))
    return np.stack([np.asarray(r['out']) for r in res.results], axis=0).astype(np.float32)
```

```python
from contextlib import ExitStack, contextmanager
import numpy as np
import concourse.bass as bass
import concourse.mybir as mybir
from concourse.bass_utils import run_bass_kernel_spmd

F32 = mybir.dt.float32
BF16 = mybir.dt.bfloat16
AF = mybir.ActivationFunctionType
ALU = mybir.AluOpType
AX = mybir.AxisListType

D = 1024
SEQ = 4096
NCTX = 256
NTOK = SEQ + NCTX
NT = NTOK // 128
DFF = 2816
NFF = DFF // 128
EPS = 1e-6
F0 = 2336
F1 = 5120

ENGS = ['pe', 'act', 'dve', 'pool', 'sp']
NDS = 12
SAME_ENGINE_SYNC = True


class Buf:
    __slots__ = ('name', 'w', 'rs', 'excl')

    def __init__(self, name=''):
        self.name = name
        self.w = None
        self.rs = {}
        self.excl = False


class T:
    def __init__(self, t, name=''):
        self.t = t
        self.b = Buf(name)

    def __getitem__(self, k):
        return self.t[k]


def _bufs(lst):
    out = []
    for x in lst:
        if x is None:
            continue
        out.append(x.b if isinstance(x, T) else x)
    return out


class Sched:
    def __init__(self, nc, stack):
        self.nc = nc
        self.sem = {e: stack.enter_context(nc.semaphore("s_" + e)) for e in ENGS}
        self.cnt = {e: 0 for e in ENGS}
        self.waited = {e: {} for e in ENGS}
        self.prog = {e: [] for e in ENGS}
        self.dsem = {q: [stack.enter_context(nc.semaphore("d_%s%d" % (q, i))) for i in range(NDS)]
                     for q in ('sp', 'pool')}
        self.dtot = {q: [0] * NDS for q in ('sp', 'pool')}
        self.dnext = {q: 0 for q in ('sp', 'pool')}
        self.st = None
        self.nalloc = 0
        self.rec = None

    def _semof(self, key):
        if key[0] == 'e':
            return self.sem[key[1]]
        return self.dsem[key[1]][key[2]]

    def _deps(self, r, w, eng=None):
        toks = {}
        same = ('e', eng)

        def add(tok, raw=False):
            if tok is None:
                return
            k, v = tok
            if k == same and not raw and eng != 'pool':
                return
            if toks.get(k, 0) < v:
                toks[k] = v
        for b in r:
            add(b.w, True)
            if b.excl:
                for k, v in b.rs.items():
                    add((k, v))
        for b in w:
            add(b.w)
            for k, v in b.rs.items():
                add((k, v))
        return toks

    def _waits(self, eng, toks):
        waits = []
        wd = self.waited[eng]
        for k, v in toks.items():
            if k == ('e', eng):
                if eng in ('pe', 'sp') or not SAME_ENGINE_SYNC:
                    continue
            if wd.get(k, 0) >= v:
                continue
            wd[k] = v
            waits.append((k, v))
        return waits

    def _commit(self, tok, r, w):
        k, v = tok
        for b in w:
            b.w = tok
            b.rs = {}
        for b in r:
            if b in w:
                continue
            if b.rs.get(k, 0) < v:
                b.rs[k] = v

    def record(self, f, *a):
        old = self.rec
        self.rec = []
        f(*a)
        out = self.rec
        self.rec = old
        return out

    def replay_zip(self, a, b):
        out = []
        for i in range(max(len(a), len(b))):
            if i < len(a):
                out.append(a[i])
            if i < len(b):
                out.append(b[i])
        self.replay(out)

    def replay(self, lst):
        for it in lst:
            if it[0] == 'op':
                self.op(it[1], it[2], it[3], it[4])
            else:
                self.dma(it[1], it[2], it[3], r=it[4], w=it[5], **it[6])

    def op(self, eng, fn, r=(), w=()):
        if self.rec is not None:
            self.rec.append(('op', eng, fn, r, w))
            return
        r = _bufs(r)
        w = _bufs(w)
        waits = self._waits(eng, self._deps(r, w, eng))
        self.cnt[eng] += 1
        tok = (('e', eng), self.cnt[eng])
        self.prog[eng].append((waits, fn, None))
        self._commit(tok, r, w)

    def dma(self, q, out, in_, r=(), w=(), **kw):
        if self.rec is not None:
            self.rec.append(('dma', q, out, in_, r, w, kw))
            return
        r = _bufs(r)
        w = _bufs(w)
        toks = self._deps(r, w)
        i = self.dnext[q]
        self.dnext[q] = (i + 1) % NDS
        key = ('d', q, i)
        if self.dtot[q][i]:
            if toks.get(key, 0) < self.dtot[q][i]:
                toks[key] = self.dtot[q][i]
        waits = self._waits(q, toks)
        self.dtot[q][i] += 16
        tok = (key, self.dtot[q][i])

        def fn(e, out=out, in_=in_, kw=kw):
            return e.dma_start(out=out, in_=in_, **kw)
        self.prog[q].append((waits, fn, i))
        self._commit(tok, r, w)

    def mm(self, out, lhsT, rhs, start, stop, r, w):
        self.op('pe', lambda e: e.matmul(out, lhsT, rhs, start=start, stop=stop), r, w)

    def tr(self, out, in_, ident, r, w):
        self.op('pe', lambda e: e.transpose(out, in_, ident), r, w)

    def act(self, out, in_, func, r, w, bias=None, scale=None):
        kw = {}
        if bias is not None:
            kw['bias'] = bias
        if scale is not None:
            kw['scale'] = scale
        self.op('act', lambda e: e.activation(out, in_, func, **kw), r, w)

    def tt(self, eng, out, in0, in1, op, r, w):
        self.op(eng, lambda e: e.tensor_tensor(out, in0, in1, op), r, w)

    def ts(self, eng, out, in0, s1, s2, op0, op1, r, w):
        if op1 is None:
            self.op(eng, lambda e: e.tensor_scalar(out, in0, s1, None, op0), r, w)
        else:
            self.op(eng, lambda e: e.tensor_scalar(out, in0, s1, s2, op0, op1), r, w)

    def stt(self, out, in0, scalar, in1, op0, op1, r, w):
        self.op('dve', lambda e: e.scalar_tensor_tensor(out, in0, scalar, in1, op0, op1), r, w)

    def copy(self, eng, out, in_, r, w):
        if eng == 'act':
            self.op('act', lambda e: e.activation(out, in_, AF.Copy), r, w)
        else:
            self.op(eng, lambda e: e.tensor_copy(out, in_), r, w)

    def sb(self, shape, dtype, name=None, stack=None):
        self.nalloc += 1
        name = "%s_%d" % (name or 'sb', self.nalloc)
        return T((stack or self.st).enter_context(self.nc.sbuf_tensor(name, list(shape), dtype)), name)

    def ps(self, shape, dtype, name=None):
        self.nalloc += 1
        name = "%s_%d" % (name or 'ps', self.nalloc)
        t = T(self.st.enter_context(self.nc.psum_tensor(name, list(shape), dtype)), name)
        t.b.excl = True
        return t

    def _emit(self, eng, e):
        for waits, fn, di in self.prog[eng]:
            for k, v in waits:
                e.wait_ge(self._semof(k), v)
            ins = fn(e)
            if di is None:
                ins.then_inc(self.sem[eng], 1)
            else:
                ins.then_inc(self.dsem[eng][di], 16)
        self.prog[eng] = []

    def _drain(self):
        for q in ('sp', 'pool'):
            toks = {}
            for i in range(NDS):
                if self.dtot[q][i]:
                    toks[('d', q, i)] = self.dtot[q][i]
            waits = self._waits(q, toks)
            if waits:
                self.prog[q].append((waits, None, 'drain'))

    def _emit2(self, eng, e):
        for waits, fn, di in self.prog[eng]:
            for k, v in waits:
                e.wait_ge(self._semof(k), v)
            if fn is None:
                continue
            ins = fn(e)
            if di is None:
                ins.then_inc(self.sem[eng], 1)
            else:
                ins.then_inc(self.dsem[eng][di], 16)
        self.prog[eng] = []

    @contextmanager
    def phase(self, name):
        with ExitStack() as st:
            self.st = st
            yield
            self._drain()
            with self.nc.Block() as block:
                block.tensor(lambda e: self._emit2('pe', e))
                block.scalar(lambda e: self._emit2('act', e))
                block.vector(lambda e: self._emit2('dve', e))
                block.gpsimd(lambda e: self._emit2('pool', e))
                block.sync(lambda e: self._emit2('sp', e))
            self.st = None


class Rot:
    def __init__(self, items):
        self.items = items
        self.i = 0

    def next(self):
        x = self.items[self.i]
        self.i = (self.i + 1) % len(self.items)
        return x


def bcast_row(ap_row, n=128):
    return ap_row.broadcast_to([n, ap_row.shape[-1]])


def host_consts():
    c = {}
    c['ident'] = np.eye(128, dtype=np.float32)
    j = np.arange(128)[:, None]
    i = np.arange(128)[None, :]
    same = (j // 64) == (i // 64)
    c['ufwd'] = (same & (j <= i)).astype(np.float32)
    c['ubwd'] = (same & (j >= i)).astype(np.float32)
    c['mfwd'] = (same & (j <= i)).astype(np.float32)
    c['mbwd'] = (same & (j >= i)).astype(np.float32)
    t = np.arange(SEQ)
    row = (t // 64).astype(np.float32)
    col = (t % 64).astype(np.float32)
    inv = (10000.0 ** (-np.arange(0, 32, 2, dtype=np.float32) / 32)).astype(np.float32)
    ang = np.concatenate([row[:, None] * inv, col[:, None] * inv], axis=-1).astype(np.float32)
    c['cos'] = np.cos(ang).astype(np.float32)
    c['sin'] = np.sin(ang).astype(np.float32)
    return c


WEIGHT_NAMES = ['mod_w', 'mod_b', 'norm_pre_mix', 'norm_post_mix', 'norm_pre_ffn', 'norm_post_ffn',
                'even_w_in', 'gla_w_gate', 'gla_b_gate', 'gla_out_norm', 'att_q_norm', 'att_k_norm',
                'even_w_out', 'odd_w_in', 'hgrn_lower_bounds', 'hgrn_out_norm', 'odd_w_out',
                'ffn_w_gate', 'ffn_w_up', 'ffn_w_down']
WEIGHT_SHAPES = {
    'mod_w': [2, D, 6 * D], 'mod_b': [2, 6 * D], 'norm_pre_mix': [2, D], 'norm_post_mix': [2, D],
    'norm_pre_ffn': [2, D], 'norm_post_ffn': [2, D], 'even_w_in': [1, D, F0],
    'gla_w_gate': [1, 2, 16, 256], 'gla_b_gate': [1, 2, 256], 'gla_out_norm': [1, 128],
    'att_q_norm': [1, 64], 'att_k_norm': [1, 64], 'even_w_out': [1, D, D], 'odd_w_in': [1, D, F1],
    'hgrn_lower_bounds': [2, 2, D], 'hgrn_out_norm': [1, 128], 'odd_w_out': [1, D, D],
    'ffn_w_gate': [2, D, DFF], 'ffn_w_up': [2, D, DFF], 'ffn_w_down': [2, DFF, D],
}
CONST_SHAPES = {'ident': [128, 128], 'ufwd': [128, 128], 'ubwd': [128, 128], 'mfwd': [128, 128],
                'mbwd': [128, 128], 'cos': [SEQ, 32], 'sin': [SEQ, 32]}


class Prog:
    def __init__(self, debug=False, upto=99, skip=(), inject=()):
        self.debug = debug
        self.upto = upto
        self.skip = set(skip)
        nc = bass.Bass("TRN2", target_bir_lowering=False)
        self.nc = nc
        self.I = {}
        self.I['xin'] = nc.dram_tensor("xin", [NTOK, D], F32, kind="ExternalInput").ap()
        self.I['cc'] = nc.dram_tensor("cc", [128, 16], F32, kind="ExternalInput").ap()
        for n in WEIGHT_NAMES:
            self.I[n] = nc.dram_tensor(n, WEIGHT_SHAPES[n], F32, kind="ExternalInput").ap()
        for n, s in CONST_SHAPES.items():
            self.I[n] = nc.dram_tensor(n, s, F32, kind="ExternalInput").ap()
        self.out = nc.dram_tensor("out", [SEQ, D], F32, kind="ExternalOutput").ap()
        kind = dict(kind="ExternalOutput") if debug else {}
        self.S = {}

        def scr(name, shape, dt):
            if name in inject:
                self.S[name] = nc.dram_tensor(name, shape, dt, kind="ExternalInput").ap()
            else:
                self.S[name] = nc.dram_tensor(name, shape, dt, **kind).ap()
        scr('MODS', [2, 6, 2, D], F32)
        scr('XR', [NTOK, D], F32)
        scr('P0', [NTOK, F0], BF16)
        scr('P1A', [NTOK, 3072], BF16)
        scr('P1B', [NTOK, 2048], F32)
        scr('OF', [NTOK, D], F32)
        scr('O', [NTOK, D], BF16)
        self.db = {n: [Buf("%s%d" % (n, t)) for t in range(NT)] for n in ['XR', 'P0', 'P1A', 'P1B', 'OF', 'O', 'OUT']}
        self.db_mods = Buf('MODS')

        with ExitStack() as stack:
            self.s = Sched(nc, stack)
            self.build()

    def load_consts(self, names):
        s = self.s
        out = {}
        for n in names:
            t = s.sb([128, 128], F32, n)
            s.dma('sp', t[:], self.I[n][:, :], w=[t])
            out[n] = t
        return out

    def load_ident_bf16(self):
        s = self.s
        t = s.sb([128, 128], BF16, 'identb')
        s.dma('pool', t[:], self.I['ident'][:, :], w=[t])
        return t

    def load_w_bf16(self, wt, src, rows, cols, c0=0, q='pool'):
        s = self.s
        for kc in range(rows // 128):
            s.dma(q, wt[:, kc, 0:cols], src[kc * 128:(kc + 1) * 128, c0:c0 + cols], w=[wt])

    def build(self):
        order = [('mods', self.phase_mods), ('proj0', lambda: self.phase_proj(0)),
                 ('gla0', lambda: self.phase_scan(0)), ('att0', self.phase_att),
                 ('outp0', lambda: self.phase_outproj(0)), ('ffn0', lambda: self.phase_ffn(0)),
                 ('proj1', lambda: self.phase_proj(1)), ('hgrn1', lambda: self.phase_scan(1)),
                 ('outp1', lambda: self.phase_outproj(1)), ('ffn1', lambda: self.phase_ffn(1))]
        self.pre = {}
        names = [n for n, _ in order]
        full = self.upto >= len(order) - 1 and not self.skip
        if full:
            self._build_full(dict(order))
            return
        for i, (name, fn) in enumerate(order):
            if i > self.upto or name in self.skip:
                continue
            if name.startswith('outp') and (i + 1) <= self.upto and ('ffn' + name[-1]) not in self.skip:
                with ExitStack() as pst:
                    l = int(name[-1])
                    self.pre['ffn%d' % l] = dict(
                        wg=self.s.sb([128, 8, DFF], BF16, 'wg', stack=pst),
                        wu=self.s.sb([128, 8, DFF], BF16, 'wu', stack=pst),
                        wd=self.s.sb([128, NFF, D], BF16, 'wd', stack=pst), loaded=False)
                    fn()
                    order[i + 1][1]()
                    self.skip.add(order[i + 1][0])
                continue
            fn()

    def _build_full(self, ph):
        sch = self.s
        I = self.I

        def ffn_w(pst):
            return dict(wg=sch.sb([128, 8, DFF], BF16, 'wg', stack=pst), wu=sch.sb([128, 8, DFF], BF16, 'wu', stack=pst),
                        wd=sch.sb([128, NFF, D], BF16, 'wd', stack=pst), loaded=False)
        with ExitStack() as pst:
            self.pre['proj0'] = dict(w=sch.sb([128, 8, F0], BF16, 'win0', stack=pst), loaded=False)
            ph['mods']()
            ph['proj0']()
        for l, mixers in ((0, ['gla0', 'att0']), (1, ['hgrn1'])):
            if l == 1:
                ph['proj1']()
            for m in mixers:
                ph[m]()
            with ExitStack() as pst:
                self.pre['ffn%d' % l] = ffn_w(pst)
                ph['outp%d' % l]()
                ph['ffn%d' % l]()

    def prefetch(self, key, src, rows, cols):
        pre = self.pre.get(key)
        if pre is not None and not pre['loaded']:
            self.load_w_bf16(pre['w'], src, rows, cols)
            pre['loaded'] = True

    def post_norm_res(self, y, xt, G, res, dst_ap, dst_buf, add_eng='pool'):
        s = self.s
        junk, ss, rstd, tmp = res['junk'], res['ss'].next(), res['rstd'].next(), res['tmp']
        if 'junk2' in res:
            junk, tmp = res['junk2'].next(), res['tmp2'].next()
        s.act(junk[:], y[:], AF.Square, r=[y], w=[junk])
        s.op('dve', lambda e: e.tensor_reduce(ss[:], junk[:], AX.X, ALU.add), r=[junk], w=[ss])
        s.act(ss[:], ss[:], AF.Sqrt, r=[ss, res['eps']], w=[ss], bias=res['eps'][:], scale=1.0 / D)
        s.op('dve', lambda e: e.reciprocal(rstd[:], ss[:]), r=[ss], w=[rstd])
        s.stt(tmp[:], y[:], rstd[:], G[:], ALU.mult, ALU.mult, r=[y, rstd, G], w=[tmp])
        s.tt(add_eng, xt[:], xt[:], tmp[:], ALU.add, r=[xt, tmp], w=[xt])
        s.dma('sp', dst_ap, xt[:], r=[xt], w=[dst_buf])

    def phase_outproj(self, l):
        s = self.s
        I = self.I
        wsrc = I['even_w_out'][0] if l == 0 else I['odd_w_out'][0]
        xsrc = I['xin'] if l == 0 else self.S['XR']
        tiles = range(NT) if l == 0 else range(2, NT)
        with s.phase('outp%d' % l):
            pre = self.pre.get('outp%d' % l)
            if pre is not None and pre['loaded']:
                w = pre['w']
            else:
                w = s.sb([128, 8, D], BF16, 'wout')
                self.load_w_bf16(w, wsrc, D, D)
            res = self.norm_res(need_hb=False)
            G = [self.load_mod_bc(l, 2, sg, 'G2') for sg in range(2)]
            res['junk2'] = Rot([res['junk'], s.sb([128, D], F32, 'junk2')])
            res['tmp2'] = Rot([res['tmp'], s.sb([128, D], F32, 'tmp2')])
            pre = self.pre.get('ffn%d' % l)
            if pre is not None:
                self.load_w_bf16(pre['wg'], I['ffn_w_gate'][l], D, DFF)
                self.load_w_bf16(pre['wu'], I['ffn_w_up'][l], D, DFF)
                self.load_w_bf16(pre['wd'], I['ffn_w_down'][l], DFF, D)
                pre['loaded'] = True
            xts = Rot([s.sb([128, D], F32, 'xt') for _ in range(5)])
            ots = Rot([s.sb([128, D], BF16, 'ot') for _ in range(3)])
            oTs = Rot([s.sb([128, 8, 128], BF16, 'oT') for _ in range(2)])
            ys = Rot([s.ps([128, D], F32, 'y') for _ in range(3)])
            tiles = list(tiles)
            ctx = {}

            def stA(t):
                xt = xts.next()
                ot = ots.next()
                s.dma('sp', xt[:], xsrc[t * 128:(t + 1) * 128, :], r=[self.db['XR'][t]] if l else [], w=[xt])
                s.dma('sp', ot[:], self.S['O'][t * 128:(t + 1) * 128, :], r=[self.db['O'][t]], w=[ot])
                ctx[t] = [xt, ot]

            def stB(t):
                xt, ot = ctx[t]
                trp = res['trp'].next()
                for kc in range(8):
                    s.tr(trp[:, kc, :], ot[:, kc * 128:(kc + 1) * 128], res['identb'][:], r=[ot, res['identb']], w=[trp])
                oT = oTs.next()
                s.copy('act', oT[:], trp[:], r=[trp], w=[oT])
                ctx[t] = [xt, oT]

            def stC(t):
                xt, oT = ctx[t]
                y = ys.next()
                for hf in range(2):
                    for kc in range(8):
                        s.mm(y[:, hf * 512:(hf + 1) * 512], oT[:, kc, :], w[:, kc, hf * 512:(hf + 1) * 512],
                             kc == 0, kc == 7, r=[oT, w], w=[y])
                ctx[t] = [xt, y]

            def stD(t):
                xt, y = ctx.pop(t)
                seg = 1 if t < 2 else 0
                self.post_norm_res(y, xt, G[seg], res, self.S['XR'][t * 128:(t + 1) * 128, :], self.db['XR'][t], add_eng='dve')
            n = len(tiles)
            rA = [s.record(stA, t) for t in tiles]
            rB = [s.record(stB, t) for t in tiles]
            rC = [s.record(stC, t) for t in tiles]
            rD = [s.record(stD, t) for t in tiles]
            s.replay(rA[0])
            s.replay(rB[0])
            s.replay(rA[1])
            pend = []
            for i in range(n + 1):
                if i < n:
                    s.replay(rC[i])
                if i + 1 < n:
                    s.replay(rB[i + 1])
                if i - 1 >= 0:
                    pend.append(rD[i - 1])
                if len(pend) == 2 or (i == n and pend):
                    s.replay_zip(pend[0], pend[1] if len(pend) > 1 else [])
                    pend = []
                if i + 2 < n:
                    s.replay(rA[i + 2])

    def phase_ffn(self, l):
        s = self.s
        I = self.I
        last = l == 1
        groups = [(t, t + 1) for t in range(0 if not last else 2, NT, 2)]
        with s.phase('ffn%d' % l):
            pre = self.pre.get('ffn%d' % l)
            if pre is not None and pre['loaded']:
                wg, wu, wd = pre['wg'], pre['wu'], pre['wd']
            else:
                wg = s.sb([128, 8, DFF], BF16, 'wg')
                wu = s.sb([128, 8, DFF], BF16, 'wu')
                wd = s.sb([128, NFF, D], BF16, 'wd')
                self.load_w_bf16(wg, I['ffn_w_gate'][l], D, DFF)
                self.load_w_bf16(wu, I['ffn_w_up'][l], D, DFF)
                self.load_w_bf16(wd, I['ffn_w_down'][l], DFF, D)
            res = self.norm_res(ntrp=1)
            A = s.sb([128, D], F32, 'A4')
            B = s.sb([128, D], F32, 'B4')
            Gs = [self.load_mod_bc(l, 5, sg, 'G5') if (sg == 0 or not last) else None for sg in range(2)]
            xts = Rot([s.sb([128, D], F32, 'xt') for _ in range(6)])
            hTs = Rot([s.sb([128, 8, 256], BF16, 'hT') for _ in range(2)])
            aTs = Rot([s.sb([128, 256], BF16, 'aT') for _ in range(4)])
            sgs = Rot([s.sb([128, 256], F32, 'sg') for _ in range(2)])
            pgs = Rot([s.ps([128, 512], F32, 'pg') for _ in range(3)])
            ys = [s.ps([128, D], F32, 'y') for _ in range(2)]
            ysb = [s.sb([128, D], F32, 'ysb') for _ in range(2)]
            ctx = {}
            ng = len(groups)

            def stNorm(gi):
                grp = groups[gi]
                seg = 1 if grp[0] < 2 else 0
                if seg != st8['seg']:
                    st8['seg'] = seg
                    for (tile_, v) in ((A, 3), (B, 4)):
                        s.dma('sp', tile_[:], bcast_row(self.S['MODS'][l, v, seg:seg + 1, :]), r=[self.db_mods], w=[tile_])
                xt_g, hb_g = [], []
                for i, t in enumerate(grp):
                    xt = xts.next()
                    xt_g.append(xt)
                    s.dma('sp', xt[:], self.S['XR'][t * 128:(t + 1) * 128, :], r=[self.db['XR'][t]], w=[xt])
                    hb_g.append(self.norm_mod_T(xt, A, B, None, None, res))
                ctx[gi] = dict(xt=xt_g, hb=hb_g)

            def stTr(gi):
                hT = hTs.next()
                for i in range(2):
                    self.transpose_T(ctx[gi]['hb'][i], hT, slice(i * 128, (i + 1) * 128), res)
                ctx[gi]['hT'] = hT

            def stPost(gi):
                grp = groups[gi]
                for i, t in enumerate(grp):
                    if last:
                        dst, dbuf = self.out[(t - 2) * 128:(t - 1) * 128, :], self.db['OUT'][t]
                    else:
                        dst, dbuf = self.S['XR'][t * 128:(t + 1) * 128, :], self.db['XR'][t]
                    self.post_norm_res(ysb[i], ctx[gi]['xt'][i], Gs[1 if grp[0] < 2 else 0], res, dst, dbuf)
                ctx.pop(gi)

            def chunk_gu(gi, c):
                hT = ctx[gi]['hT']
                pg = pgs.next()
                for kc in range(8):
                    s.mm(pg[:, 0:256], wg[:, kc, c * 128:(c + 1) * 128], hT[:, kc, :], kc == 0, kc == 7, r=[wg, hT], w=[pg])
                for kc in range(8):
                    s.mm(pg[:, 256:512], wu[:, kc, c * 128:(c + 1) * 128], hT[:, kc, :], kc == 0, kc == 7, r=[wu, hT], w=[pg])
                return pg

            def chunk_act(pg):
                sg = sgs.next()
                aT = aTs.next()
                s.act(sg[:], pg[:, 0:256], AF.Silu, r=[pg], w=[sg])
                s.tt('dve', aT[:], sg[:], pg[:, 256:512], ALU.mult, r=[sg, pg], w=[aT])
                return aT

            def chunk_down(pc, aT):
                for i in range(2):
                    for hf in range(2):
                        s.mm(ys[i][:, hf * 512:(hf + 1) * 512], aT[:, i * 128:(i + 1) * 128],
                             wd[:, pc, hf * 512:(hf + 1) * 512], pc == 0, pc == NFF - 1, r=[aT, wd], w=[ys[i]])
            st8 = {'seg': None}
            rN = [s.record(stNorm, gi) for gi in range(ng)]
            s.replay(rN[0])
            stTr(0)
            for gi in range(ng):
                pend = []
                for c in range(NFF + 2):
                    if c < NFF:
                        pg = chunk_gu(gi, c)
                    if pend and (len(pend) == 2 or c >= NFF):
                        chunk_down(*pend.pop(0))
                    if c < NFF:
                        pend.append((c, chunk_act(pg)))
                    if c == 1 and gi >= 1:
                        stPost(gi - 1)
                    if c == 6 and gi + 1 < ng:
                        s.replay(rN[gi + 1])
                    if c == 17 and gi + 1 < ng:
                        stTr(gi + 1)
                for i in range(2):
                    s.copy('act', ysb[i][:], ys[i][:], r=[ys[i]], w=[ysb[i]])
            stPost(ng - 1)

    def phase_scan(self, l):
        s = self.s
        I = self.I
        gla = l == 0
        H = 4 if gla else 8
        NB = 2 if gla else 8
        F = NB * 128
        HV = H * 128
        NG = 1 if gla else 2
        qscale = 0.125 if gla else 1.0
        usc = -1.0 / 16.0 if gla else 1.0
        P16 = self.S['P0'] if gla else self.S['P1A']
        p16b = self.db['P0'] if gla else self.db['P1A']
        with s.phase('scan%d' % l):
            identb = self.load_ident_bf16()
            cst = self.load_consts(['ufwd', 'ubwd', 'mfwd', 'mbwd'])
            U = [s.sb([128, 128], F32, 'U') for _ in range(2)]
            MK = [s.sb([128, 128], F32, 'MK') for _ in range(2)]
            for d, (un, mn) in enumerate((('ufwd', 'mfwd'), ('ubwd', 'mbwd'))):
                s.ts('dve', U[d][:], cst[un][:], usc, None, ALU.mult, None, r=[cst[un]], w=[U[d]])
                s.copy('dve', MK[d][:], cst[mn][:], r=[cst[mn]], w=[MK[d]])
            gain = s.sb([128, 128], F32, 'gain')
            s.dma('sp', gain[:], bcast_row((I['gla_out_norm'] if gla else I['hgrn_out_norm'])[0:1, :]), w=[gain])
            eps = s.sb([128, 1], F32, 'eps')
            s.op('dve', lambda e: e.memset(eps[:], EPS), w=[eps])
            ones1 = s.sb([128, 1], F32, 'ones1')
            s.op('dve', lambda e: e.memset(ones1[:], 1.0), w=[ones1])
            if not gla:
                self.prefetch('outp1', I['odd_w_out'][0], D, D)
            if gla:
                WG = s.sb([33, 512], BF16, 'WG')
                s.op('dve', lambda e: e.memset(WG[:], 0.0), w=[WG])
                s.dma('pool', WG[0:16, 0:256], I['gla_w_gate'][0, 0], w=[WG])
                s.dma('pool', WG[16:32, 256:512], I['gla_w_gate'][0, 1], w=[WG])
                s.dma('pool', WG[32:33, :], I['gla_b_gate'][0:1].rearrange("a d f -> a (d f)"), w=[WG])
                zT = s.sb([33, 128], BF16, 'zT')
                s.op('dve', lambda e: e.memset(zT[:], 1.0), w=[zT])
            if gla:
                GT = [s.ps([128, 4, 128], F32, 'GT') for _ in range(2)]
                ATps = Rot([s.ps([128, 4, 128], F32, 'ATp')])
            else:
                GT = [s.ps([128, 4, 128], F32, 'GT') for _ in range(2)]
                ATps = Rot([s.ps([128, 4, 128], F32, 'ATp')])
            TR = Rot([s.ps([128, 8, 128], BF16, 'TR') for _ in range(2)])
            OP = s.ps([128, 4, 128], F32, 'OP')
            DS = Rot([s.ps([128, 4, 128], F32, 'DS') for _ in range(2)])
            raw16 = Rot([s.sb([128, 1024 if gla else 2048], BF16, 'raw16') for _ in range(3)])
            if gla:
                gzs = Rot([s.sb([128, 32], BF16, 'gz') for _ in range(3)])
                ex = s.sb([128, 256], F32, 'ex')
            else:
                f32s = Rot([s.sb([128, D], F32, 'f32') for _ in range(2)])
                tt_ = s.sb([128, D], F32, 'tt')
                qs = Rot([s.sb([128, D], BF16, 'qs') for _ in range(2)])
                ks = Rot([s.sb([128, D], BF16, 'ks') for _ in range(3)])
            gsrc = Rot([s.sb([128, F], F32, 'gsrc') for _ in range(3)])
            Ep = Rot([s.sb([128, NB, 128], F32, 'Ep') for _ in range(2)])
            Em = Rot([s.sb([128, NB, 128], F32, 'Em') for _ in range(2)])
            QA = Rot([s.sb([128, H, 128], BF16, 'QA') for _ in range(2)])
            QB = Rot([s.sb([128, H, 128], BF16, 'QB') for _ in range(2)])
            for q_ in QA.items + QB.items:
                s.op('pool', lambda e, q_=q_: e.memset(q_[:], 0.0), w=[q_])
            KT = Rot([s.sb([128, NB, 128], BF16, 'KT') for _ in range(2)])
            KTM = Rot([s.sb([128, NB, 128], BF16, 'KTM') for _ in range(2)])
            DEC = Rot([s.sb([128, NB, 2], F32, 'DEC') for _ in range(2)])
            ATm = Rot([s.sb([128, 4, 128], BF16, 'ATm') for _ in range(2)])
            Sf = Rot([s.sb([128, NB, 128], F32, 'Sf') for _ in range(3)])
            Sb = Rot([s.sb([128, NB, 128], BF16, 'Sb') for _ in range(4)])
            stmp = Rot([s.sb([128, 4, 128], F32, 'stmp') for _ in range(2)])
            ofs = Rot([s.sb([128, HV], F32, 'of') for _ in range(2)])
            osum = s.sb([128, H, 128], F32, 'osum')
            junk = s.sb([128, H, 128], F32, 'junk')
            ogs = Rot([s.sb([128, HV], BF16, 'og') for _ in range(2)])
            sgt = s.sb([128, HV], F32, 'sgt')
            ss = Rot([s.sb([128, H], F32, 'ss') for _ in range(2)])
            rstd = Rot([s.sb([128, H], F32, 'rstd') for _ in range(2)])
            ob = Rot([s.sb([128, HV], BF16, 'ob') for _ in range(2)])

            import os as _os
            for d in range(int(_os.environ.get('SCAN_DIRS', '2'))):
                fwd = d == 0
                order = list(range(NT)) if fwd else [1, 0] + list(range(NT - 1, 1, -1))
                first = 0 if fwd else 1
                S_prev = Sf.next()
                Sb_prev = Sb.next()
                s.op('dve', lambda e, S_prev=S_prev: e.memset(S_prev[:], 0.0), w=[S_prev])
                s.op('pool', lambda e, Sb_prev=Sb_prev: e.memset(Sb_prev[:], 0.0), w=[Sb_prev])
                order = order[:int(_os.environ.get('SCAN_TILES', '99'))]
                ctx = {}
                stt8 = {'S': S_prev, 'Sb': Sb_prev}

                def stA1(t):
                    c_ = ctx.setdefault(t, {})
                    rows = slice(t * 128, (t + 1) * 128)
                    r16 = raw16.next()
                    if gla:
                        s.dma('sp', r16[:], P16[rows, 0:1024], r=[p16b[t]], w=[r16])
                        gz = gzs.next()
                        s.dma('sp', gz[:], P16[rows, 1536:1568], r=[p16b[t]], w=[gz])
                        c_.update(q_tm=r16, k_tm=r16, v_tm=r16, q0=0, k0=256, v0=512)
                    else:
                        s.dma('sp', r16[:], P16[rows, 0:2048], r=[p16b[t]], w=[r16])
                        f32 = f32s.next()
                        s.dma('sp', f32[:], self.S['P1B'][rows, d * 1024:(d + 1) * 1024], r=[self.db['P1B'][t]], w=[f32])
                        c_.update(v_tm=r16, v0=1024)
                    gs = gsrc.next()
                    c_['gs'] = gs
                    if gla:
                        trp = TR.next()
                        s.tr(trp[0:32, 0, :], gz[:], identb[:], r=[gz, identb], w=[trp])
                        s.copy('act', zT[0:32, :], trp[0:32, 0, :], r=[trp], w=[zT])
                        lg = GT[1]
                        lgv = lg[:].rearrange("p a b -> p (a b)")[:, 0:256]
                        s.mm(lgv, zT[:], WG[:, d * 256:(d + 1) * 256], True, True, r=[zT, WG], w=[lg])
                        s.act(ex[:], lgv, AF.Exp, r=[lg], w=[ex], scale=-1.0)
                        s.act(gs[:], ex[:], AF.Ln, r=[ex], w=[gs], bias=1.0)
                    else:
                        s.act(gs[:], f32[:], AF.Ln, r=[f32], w=[gs])
                        k_tm = ks.next()
                        s.act(k_tm[:], f32[:], AF.Identity, r=[f32, ones1], w=[k_tm], bias=ones1[:], scale=-1.0)
                        c_.update(q_tm=r16, k_tm=k_tm, q0=0, k0=0)

                def stA2(t):
                    c_ = ctx[t]
                    gs = c_['gs']
                    ep, em, dec = Ep.next(), Em.next(), DEC.next()
                    for g in range((NB + 3) // 4):
                        nb_ = min(4, NB - 4 * g)
                        for fb in range(4 * g, 4 * g + nb_):
                            s.mm(GT[g][:, fb % 4, :], gs[:, fb * 128:(fb + 1) * 128], U[d][:], True, True,
                                 r=[gs, U[d]], w=[GT[g]])
                        s.act(ep[:, 4 * g:4 * g + nb_, :], GT[g][:, 0:nb_, :], AF.Exp, r=[GT[g]], w=[ep])
                        s.act(em[:, 4 * g:4 * g + nb_, :], GT[g][:, 0:nb_, :], AF.Exp, r=[GT[g]], w=[em], scale=-1.0)
                    c_off = 63 if fwd else 0
                    s.copy('act', dec[:], ep[:, :, c_off::64], r=[ep], w=[dec])
                    c_.update(ep=ep, em=em, dec=dec)

                def stA3(t):
                    c_ = ctx[t]
                    ep, em, q_tm, k_tm, q0, k0 = c_['ep'], c_['em'], c_['q_tm'], c_['k_tm'], c_['q0'], c_['k0']
                    qa, qb, kt, ktm = QA.next(), QB.next(), KT.next(), KTM.next()
                    trq = TR.next()
                    for fb in range(NB):
                        s.tr(trq[:, fb, :], q_tm[:, q0 + fb * 128:q0 + (fb + 1) * 128], identb[:], r=[q_tm, identb], w=[trq])
                    if gla:
                        for (pr_, par) in ((slice(0, 64), 0), (slice(64, 128), 1)):
                            s.stt(qa[pr_, par::2, 0:64], trq[pr_, 0:NB, 0:64], qscale, ep[pr_, :, 0:64], ALU.mult, ALU.mult, r=[trq, ep], w=[qa])
                            s.stt(qb[pr_, par::2, 64:128], trq[pr_, 0:NB, 64:128], qscale, ep[pr_, :, 64:128], ALU.mult, ALU.mult, r=[trq, ep], w=[qb])
                    else:
                        s.stt(qa[:, :, 0:64], trq[:, 0:NB, 0:64], qscale, ep[:, :, 0:64], ALU.mult, ALU.mult, r=[trq, ep], w=[qa])
                        s.stt(qb[:, :, 64:128], trq[:, 0:NB, 64:128], qscale, ep[:, :, 64:128], ALU.mult, ALU.mult, r=[trq, ep], w=[qb])
                    trk = TR.next()
                    for fb in range(NB):
                        s.tr(trk[:, fb, :], k_tm[:, k0 + fb * 128:k0 + (fb + 1) * 128], identb[:], r=[k_tm, identb], w=[trk])
                    s.tt('dve', kt[:], trk[:, 0:NB, :], em[:], ALU.mult, r=[trk, em], w=[kt])
                    trm = TR.next()
                    for fb in range(NB):
                        s.tr(trm[:, fb, :], kt[:, fb, :], identb[:], r=[kt, identb], w=[trm])
                    s.copy('act', ktm[:], trm[:, 0:NB, :], r=[trm], w=[ktm])
                    c_.update(qa=qa, qb=qb, kt=kt, ktm=ktm)

                def state_step(t, which):
                    c_ = ctx[t]
                    ktm, v_tm, v0, dec = c_['ktm'], c_['v_tm'], c_['v0'], c_['dec']
                    c = first if which == 0 else 1 - first
                    S_in = stt8['S']
                    S_out, Sb_out = Sf.next(), Sb.next()
                    if which == 0:
                        c_['Sb_prev'] = stt8['Sb']
                        c_['Sb_mid'] = Sb_out
                    crow = slice(c * 64, (c + 1) * 64)
                    grp_ = []
                    for g in range(NG):
                        ds = DS.next()
                        st_ = stmp.next()
                        if gla:
                            dsv = ds[:].rearrange("p a b -> p (a b)")
                            for fb in range(2):
                                s.mm(dsv[:, fb * 256:(fb + 1) * 256], ktm[crow, fb, :],
                                     v_tm[crow, v0 + fb * 256:v0 + (fb + 1) * 256], True, True, r=[ktm, v_tm], w=[ds])
                            dsp = ds[:].rearrange("p (f a) b -> p f a b", a=2)
                            s.tt('dve', st_[0:64, 0:2, :], dsp[0:64, :, 0, :], S_in[0:64, :, :], ALU.add, r=[ds, S_in], w=[st_])
                            s.tt('dve', st_[64:128, 0:2, :], dsp[64:128, :, 1, :], S_in[64:128, :, :], ALU.add, r=[ds, S_in], w=[st_])
                            nb_, b0_ = 2, 0
                        else:
                            for hh in range(4):
                                h = 4 * g + hh
                                s.mm(ds[:, hh, :], ktm[crow, h, :], v_tm[crow, v0 + h * 128:v0 + (h + 1) * 128],
                                     True, True, r=[ktm, v_tm], w=[ds])
                            s.tt('dve', st_[:], ds[:], S_in[:, 4 * g:4 * g + 4, :], ALU.add, r=[ds, S_in], w=[st_])
                            nb_, b0_ = 4, 4 * g
                        grp_.append((st_, nb_, b0_))
                    for (st_, nb_, b0_) in grp_:
                        dcb = dec[:, b0_:b0_ + nb_, c:c + 1].broadcast_to([128, nb_, 128])
                        s.tt('dve', S_out[:, b0_:b0_ + nb_, :], st_[:, 0:nb_, :], dcb, ALU.mult, r=[st_, dec], w=[S_out])
                    for (st_, nb_, b0_) in grp_:
                        s.copy('dve', Sb_out[:, b0_:b0_ + nb_, :], S_out[:, b0_:b0_ + nb_, :], r=[S_out], w=[Sb_out])
                    stt8['S'], stt8['Sb'] = S_out, Sb_out

                def stB1(t):
                    state_step(t, 0)

                def stB2(t):
                    state_step(t, 1)

                def stB3(t):
                    c_ = ctx.pop(t)
                    need_o = not (l == 1 and t < 2)
                    if not need_o:
                        return
                    rows = slice(t * 128, (t + 1) * 128)
                    qa, qb, kt, v_tm, v0 = c_['qa'], c_['qb'], c_['kt'], c_['v_tm'], c_['v0']
                    Sb_prev, Sb_mid = c_['Sb_prev'], c_['Sb_mid']
                    of = ofs.next()
                    if not fwd:
                        s.dma('sp', of[:], self.S['OF'][rows, 0:HV], r=[self.db['OF'][t]], w=[of])
                    SA, SB_ = (Sb_prev, Sb_mid) if fwd else (Sb_mid, Sb_prev)
                    atms = []
                    for g in range(NG):
                        ATp = ATps.next()
                        for hh in range(4):
                            h = 4 * g + hh
                            fb = h // 2 if gla else h
                            s.mm(ATp[:, hh, 0:64], kt[:, fb, :], qa[:, h, 0:64], True, True, r=[kt, qa], w=[ATp])
                            s.mm(ATp[:, hh, 64:128], kt[:, fb, :], qb[:, h, 64:128], True, True, r=[kt, qb], w=[ATp])
                        atm = ATm.next()
                        mkb = MK[d][:].unsqueeze(1).broadcast_to([128, 4, 128])
                        s.tt('dve', atm[:], ATp[:], mkb, ALU.mult, r=[ATp, MK[d]], w=[atm])
                        atms.append(atm)
                    for g in range(NG):
                        atm = atms[g]
                        for hh in range(4):
                            h = 4 * g + hh
                            fb = h // 2 if gla else h
                            s.mm(OP[:, hh, :], atm[:, hh, :], v_tm[:, v0 + h * 128:v0 + (h + 1) * 128], True, False,
                                 r=[atm, v_tm], w=[OP])
                            s.mm(OP[:, hh, :], qa[:, h, :], SA[:, fb, :], False, False, r=[qa, SA], w=[OP])
                            s.mm(OP[:, hh, :], qb[:, h, :], SB_[:, fb, :], False, True, r=[qb, SB_], w=[OP])
                        opv = OP[:].rearrange("p a b -> p (a b)")
                        if fwd:
                            s.copy('act', of[:, g * 512:(g + 1) * 512], opv, r=[OP], w=[of])
                        else:
                            s.tt('dve', osum[:, 4 * g:4 * g + 4, :], OP[:], of[:, g * 512:(g + 1) * 512].rearrange("p (a b) -> p a b", b=128),
                                 ALU.add, r=[OP, of], w=[osum])
                    if fwd:
                        s.dma('sp', self.S['OF'][rows, 0:HV], of[:], r=[of], w=[self.db['OF'][t]])
                    else:
                        og = ogs.next()
                        ogc = 1024 if gla else 2048
                        s.dma('sp', og[:], P16[rows, ogc:ogc + HV], r=[p16b[t]], w=[og])
                        ss_, rs_ = ss.next(), rstd.next()
                        s.tt('pool', junk[:], osum[:], osum[:], ALU.mult, r=[osum], w=[junk])
                        s.op('dve', lambda e, ss_=ss_: e.tensor_reduce(ss_[:], junk[:], AX.X, ALU.add), r=[junk], w=[ss_])
                        s.act(ss_[:], ss_[:], AF.Sqrt, r=[ss_, eps], w=[ss_], bias=eps[:], scale=1.0 / 128)
                        s.op('dve', lambda e, ss_=ss_, rs_=rs_: e.reciprocal(rs_[:], ss_[:]), r=[ss_], w=[rs_])
                        s.tt('dve', osum[:], osum[:], rs_[:].unsqueeze(2).broadcast_to([128, H, 128]), ALU.mult, r=[osum, rs_], w=[osum])
                        s.tt('pool', osum[:], osum[:], gain[:].unsqueeze(1).broadcast_to([128, H, 128]), ALU.mult, r=[osum, gain], w=[osum])
                        o_ = ob.next()
                        s.tt('dve', o_[:], osum[:].rearrange("p a b -> p (a b)"), og[:], ALU.mult, r=[osum, og], w=[o_])
                        s.dma('sp', self.S['O'][rows, 0:HV], o_[:], r=[o_], w=[self.db['O'][t]])
                stages = [stA1, stA2, stA3, stB1, stB2, stB3]
                recs = [[] for _ in stages]
                for t in order:
                    for k, f in enumerate(stages):
                        recs[k].append(s.record(f, t))
                n_ = len(order)
                s.replay(recs[0][0])
                s.replay(recs[0][1])
                s.replay(recs[1][0])
                s.replay(recs[2][0])
                for i in range(n_):
                    if i + 2 < n_:
                        s.replay(recs[0][i + 2])
                    s.replay(recs[3][i])
                    if i + 1 < n_:
                        s.replay(recs[1][i + 1])
                    s.replay(recs[4][i])
                    if i + 1 < n_:
                        s.replay(recs[2][i + 1])
                    s.replay(recs[5][i])

    def phase_att(self):
        s = self.s
        I = self.I
        P0 = self.S['P0']
        QC, KC, VC = 1568, 2080, 2208
        with s.phase('att'):
            identb = self.load_ident_bf16()
            identf = s.sb([128, 128], F32, 'identf')
            s.dma('sp', identf[:], I['ident'][:, :], w=[identf])
            eps = s.sb([128, 1], F32, 'eps')
            s.op('dve', lambda e: e.memset(eps[:], EPS), w=[eps])
            GQK = s.sb([128, 10, 64], F32, 'GQK')
            for h in range(10):
                src = I['att_q_norm'] if h < 8 else I['att_k_norm']
                s.dma('sp', GQK[:, h, :], bcast_row(src[0:1, :]), w=[GQK])
            s.ts('dve', GQK[:, 0:8, :], GQK[:, 0:8, :], 0.125, None, ALU.mult, None, r=[GQK], w=[GQK])
            QT = s.sb([128, 4, NTOK], BF16, 'QT')
            KT = [s.sb([128, NTOK], BF16, 'KT%d' % kv) for kv in range(2)]
            for kv in range(2):
                s.op('pool', lambda e, kv=kv: e.memset(KT[kv][:], 0.0), w=[KT[kv]])
            VA = s.sb([128, NT, 2, 66], BF16, 'VA')
            s.op('pool', lambda e: e.memset(VA[:], 1.0), w=[VA])
            self.prefetch('outp0', I['even_w_out'][0], D, D)
            raws = Rot([s.sb([128, 10, 64], BF16, 'raw') for _ in range(2)])
            vraws = Rot([s.sb([128, 2, 64], BF16, 'vraw') for _ in range(2)])
            coss = Rot([s.sb([128, 32], F32, 'cos') for _ in range(2)])
            sins = Rot([s.sb([128, 32], F32, 'sin') for _ in range(2)])
            junk = s.sb([128, 10, 64], F32, 'junk')
            xn = s.sb([128, 10, 32, 2], F32, 'xn')
            tA = s.sb([128, 10, 32], F32, 'tA')
            tB = s.sb([128, 10, 32], F32, 'tB')
            tC = s.sb([128, 10, 32], F32, 'tC')
            tD = s.sb([128, 10, 32], F32, 'tD')
            ss = Rot([s.sb([128, 10], F32, 'ss') for _ in range(2)])
            rstd = Rot([s.sb([128, 10], F32, 'rstd') for _ in range(2)])
            rqs = Rot([s.sb([128, 10, 32, 2], BF16, 'rq') for _ in range(2)])
            trps = Rot([s.ps([128, 8, 128], BF16, 'trp') for _ in range(1)])
            import os as _os
            _stop = int(_os.environ.get('ATT_STOP', '9'))
            _pt = int(_os.environ.get('ATT_PTILES', '99'))
            _qt = int(_os.environ.get('ATT_QTILES', '99'))
            for t in range(min(NT, _pt)):
                rows = slice(t * 128, (t + 1) * 128)
                raw, vraw = raws.next(), vraws.next()
                for j in range(2):
                    s.dma('sp', raw[:, j:8:2, :], P0[rows, QC + j * 256:QC + (j + 1) * 256].rearrange("p (i d) -> p i d", d=64),
                          r=[self.db['P0'][t]], w=[raw])
                s.dma('sp', raw[:, 8:10, :].rearrange("p a b -> p (a b)"), P0[rows, KC:VC], r=[self.db['P0'][t]], w=[raw])
                s.dma('sp', vraw[:].rearrange("p a b -> p (a b)"), P0[rows, VC:VC + 128], r=[self.db['P0'][t]], w=[vraw])
                s.copy('pool', VA[:, t, :, 0:64], vraw[:], r=[vraw], w=[VA])
                ss_, rs_ = ss.next(), rstd.next()
                s.tt('pool', junk[:], raw[:], raw[:], ALU.mult, r=[raw], w=[junk])
                s.op('dve', lambda e, ss_=ss_: e.tensor_reduce(ss_[:], junk[:], AX.X, ALU.add), r=[junk], w=[ss_])
                s.act(ss_[:], ss_[:], AF.Sqrt, r=[ss_, eps], w=[ss_], bias=eps[:], scale=1.0 / 64)
                s.op('dve', lambda e, ss_=ss_, rs_=rs_: e.reciprocal(rs_[:], ss_[:]), r=[ss_], w=[rs_])
                xnv = xn[:].rearrange("p a b c -> p a (b c)")
                s.tt('dve', xnv, raw[:], rs_[:].unsqueeze(2).broadcast_to([128, 10, 64]), ALU.mult, r=[raw, rs_], w=[xn])
                rq = rqs.next()
                if t < 2:
                    s.tt('dve', rq[:].rearrange("p a b c -> p a (b c)"), xnv, GQK[:], ALU.mult, r=[xn, GQK], w=[rq])
                else:
                    s.tt('pool', xnv, xnv, GQK[:], ALU.mult, r=[xn, GQK], w=[xn])
                    cs, sn = coss.next(), sins.next()
                    s.dma('sp', cs[:], I['cos'][(t - 2) * 128:(t - 1) * 128, :], w=[cs])
                    s.dma('sp', sn[:], I['sin'][(t - 2) * 128:(t - 1) * 128, :], w=[sn])
                    cb = cs[:].unsqueeze(1).broadcast_to([128, 10, 32])
                    sb_ = sn[:].unsqueeze(1).broadcast_to([128, 10, 32])
                    x0, x1 = xn[:, :, :, 0], xn[:, :, :, 1]
                    s.tt('dve', tA[:], x0, cb, ALU.mult, r=[xn, cs], w=[tA])
                    s.tt('pool', tB[:], x1, sb_, ALU.mult, r=[xn, sn], w=[tB])
                    s.tt('dve', rq[:, :, :, 0], tA[:], tB[:], ALU.subtract, r=[tA, tB], w=[rq])
                    s.tt('pool', tC[:], x0, sb_, ALU.mult, r=[xn, sn], w=[tC])
                    s.tt('dve', tD[:], x1, cb, ALU.mult, r=[xn, cs], w=[tD])
                    s.tt('dve', rq[:, :, :, 1], tC[:], tD[:], ALU.add, r=[tC, tD], w=[rq])
                if _stop < 1:
                    continue
                rqv = rq[:].rearrange("p a b c -> p a (b c)")
                trp = trps.next()
                for i in range(4):
                    s.tr(trp[:, i, :], rqv[:, 2 * i:2 * i + 2, :], identb[:], r=[rq, identb], w=[trp])
                s.tr(trp[:, 4, :], rqv[:, 8:10, :], identb[:], r=[rq, identb], w=[trp])
                s.copy('act', QT[:, :, rows], trp[:, 0:4, :], r=[trp], w=[QT])
                s.copy('dve', KT[0][0:64, rows], trp[0:64, 4, :], r=[trp], w=[KT[0]])
                s.copy('dve', KT[1][64:128, rows], trp[64:128, 4, :], r=[trp], w=[KT[1]])
            SPs = Rot([s.ps([128, 2, 512], F32, 'SP') for _ in range(2)])
            ACCs = Rot([s.ps([66, 512], F32, 'ACC') for _ in range(2)])
            TP = s.ps([128, 4, 128], F32, 'TP')
            PTs = Rot([s.sb([128, 2, 512], BF16, 'PT') for _ in range(3)])
            accs = Rot([s.sb([66, 512], F32, 'accs') for _ in range(3)])
            rcp = Rot([s.sb([128, 4], F32, 'rcp') for _ in range(2)])
            obs = Rot([s.sb([128, 512], BF16, 'ob') for _ in range(3)])
            items = []
            for qt in range(min(NT, _qt) if _stop >= 2 else 0):
                keys = [0, 1] if qt < 2 else list(range(NT))
                for kv in range(2):
                    for n in range(0, len(keys), 2):
                        items.append((qt, kv, keys[n:n + 2], n == 0, n + 2 >= len(keys)))
            ctx = {}
            fin = {}
            st8 = {'ob': None, 'acc': None}

            def stQK(i):
                qt, kv, kts, first, last = items[i]
                rows = slice(qt * 128, (qt + 1) * 128)
                sp = SPs.next()
                for j, kt in enumerate(kts):
                    s.mm(sp[:, j, :], KT[kv][:, kt * 128:(kt + 1) * 128], QT[:, :, rows], True, True, r=[KT[kv], QT], w=[sp])
                ctx[i] = sp

            def stEXP(i):
                sp = ctx[i]
                pt = PTs.next()
                s.act(pt[:], sp[:], AF.Exp, r=[sp], w=[pt])
                ctx[i] = pt

            def stPV(i):
                qt, kv, kts, first, last = items[i]
                rows = slice(qt * 128, (qt + 1) * 128)
                pt = ctx.pop(i)
                if first:
                    st8['acc'] = ACCs.next()
                    if kv == 0:
                        st8['ob'] = obs.next()
                acc, ob = st8['acc'], st8['ob']
                for j, kt in enumerate(kts):
                    s.mm(acc[:], VA[:, kt, kv, :], pt[:, j, :], first and j == 0, last and j == len(kts) - 1, r=[VA, pt], w=[acc])
                if not last:
                    return
                ac = accs.next()
                s.copy('dve', ac[:], acc[:], r=[acc], w=[ac])
                fin[i] = (ac, ob)

            def stFIN(i):
                if i not in fin:
                    return
                qt, kv, kts, first, last = items[i]
                rows = slice(qt * 128, (qt + 1) * 128)
                ac, ob = fin.pop(i)
                for h in range(4):
                    s.tr(TP[:, h, 0:66], ac[:, h * 128:(h + 1) * 128], identf[0:66, 0:66], r=[ac, identf], w=[TP])
                rc = rcp.next()
                s.op('dve', lambda e, rc=rc: e.reciprocal(rc[:], TP[:, :, 64]), r=[TP], w=[rc])
                s.tt('dve', ob[:, kv * 256:(kv + 1) * 256].rearrange("p (a b) -> p a b", b=64), TP[:, :, 0:64],
                     rc[:].unsqueeze(2).broadcast_to([128, 4, 64]), ALU.mult, r=[TP, rc], w=[ob])
                if kv == 1:
                    s.dma('sp', self.S['O'][rows, 512:1024], ob[:], r=[ob], w=[self.db['O'][qt]])
            n_it = len(items)
            rQ = [s.record(stQK, i) for i in range(n_it)]
            rE = [s.record(stEXP, i) for i in range(n_it)]
            rP = [s.record(stPV, i) for i in range(n_it)]
            rF = [s.record(stFIN, i) for i in range(n_it)]
            for step in range(n_it + 5):
                if step < n_it:
                    s.replay(rQ[step])
                if 0 <= step - 1 < n_it:
                    s.replay(rE[step - 1])
                if 0 <= step - 2 < n_it:
                    s.replay(rP[step - 2])
                if 0 <= step - 4 < n_it:
                    s.replay(rF[step - 4])

    def phase_mods(self):
        s = self.s
        I = self.I
        with s.phase('mods'):
            self.prefetch('proj0', I['even_w_in'][0], D, F0)
            cc = s.sb([128, 16], F32, 'cc')
            sc = s.sb([128, 8, 2], F32, 'sc')
            s.dma('sp', cc[:], I['cc'][:, :], w=[cc])
            s.act(sc[:].rearrange("p a b -> p (a b)"), cc[:], AF.Silu, r=[cc], w=[sc])
            wb = Rot([s.sb([128, 8, 512], F32, 'mw') for _ in range(3)])
            pb = Rot([s.ps([2, 512], F32, 'mp') for _ in range(2)])
            mrow = s.sb([2, 6 * D], F32, 'mrow')
            bias = s.sb([2, 6 * D], F32, 'mbias')
            der = s.sb([2, 6, D], F32, 'der')
            gains = {gn: s.sb([2, D], F32, gn) for gn in ['norm_pre_mix', 'norm_post_mix', 'norm_pre_ffn', 'norm_post_ffn']}
            for l in range(2):
                s.dma('sp', bias[:], bcast_row(I['mod_b'][l:l + 1, :], 2), w=[bias])
                for gn in gains:
                    s.dma('sp', gains[gn][:], bcast_row(I[gn][l:l + 1, :], 2), w=[gains[gn]])
                wsrc = I['mod_w'][l].rearrange("(kc p) f -> p kc f", p=128)
                for nb in range(12):
                    w = wb.next()
                    s.dma('sp', w[:], wsrc[:, :, nb * 512:(nb + 1) * 512], w=[w])
                    p = pb.next()
                    for kc in range(8):
                        s.mm(p[:], sc[:, kc, :], w[:, kc, :], kc == 0, kc == 7, r=[sc, w], w=[p])
                    s.tt('dve', mrow[:, nb * 512:(nb + 1) * 512], p[:], bias[:, nb * 512:(nb + 1) * 512],
                         ALU.add, r=[p, bias], w=[mrow])
                def m(i):
                    return mrow[:, i * D:(i + 1) * D]
                s.stt(der[:, 0, :], m(1), 1.0, gains['norm_pre_mix'][:], ALU.add, ALU.mult,
                      r=[mrow, gains['norm_pre_mix']], w=[der])
                s.copy('dve', der[:, 1, :], m(0), r=[mrow], w=[der])
                s.tt('dve', der[:, 2, :], m(2), gains['norm_post_mix'][:], ALU.mult,
                     r=[mrow, gains['norm_post_mix']], w=[der])
                s.stt(der[:, 3, :], m(4), 1.0, gains['norm_pre_ffn'][:], ALU.add, ALU.mult,
                      r=[mrow, gains['norm_pre_ffn']], w=[der])
                s.copy('dve', der[:, 4, :], m(3), r=[mrow], w=[der])
                s.tt('dve', der[:, 5, :], m(5), gains['norm_post_ffn'][:], ALU.mult,
                     r=[mrow, gains['norm_post_ffn']], w=[der])
                s.dma('sp', self.S['MODS'][l].rearrange("v s d -> s v d"), der[:], r=[der], w=[self.db_mods])

    def load_mod_bc(self, l, v, seg, name):
        s = self.s
        t = s.sb([128, D], F32, name)
        s.dma('sp', t[:], bcast_row(self.S['MODS'][l, v, seg:seg + 1, :]), r=[self.db_mods], w=[t])
        return t

    def norm_mod_T(self, xt, A, B, hT, hT_slice, res):
        s = self.s
        junk, ss, rstd, tmp, hb = (res['junk'], res['ss'].next(), res['rstd'].next(), res['tmp'], res['hb'].next())
        s.tt('pool', junk[:], xt[:], xt[:], ALU.mult, r=[xt], w=[junk])
        s.op('dve', lambda e: e.tensor_reduce(ss[:], junk[:], AX.X, ALU.add), r=[junk], w=[ss])
        s.act(ss[:], ss[:], AF.Sqrt, r=[ss, res['eps']], w=[ss], bias=res['eps'][:], scale=1.0 / D)
        s.op('dve', lambda e: e.reciprocal(rstd[:], ss[:]), r=[ss], w=[rstd])
        s.stt(tmp[:], xt[:], rstd[:], A[:], ALU.mult, ALU.mult, r=[xt, rstd, A], w=[tmp])
        s.tt('dve', hb[:], tmp[:], B[:], ALU.add, r=[tmp, B], w=[hb])
        if hT is None:
            return hb
        self.transpose_T(hb, hT, hT_slice, res)

    def transpose_T(self, hb, hT, hT_slice, res):
        s = self.s
        trp, identb = res['trp'].next(), res['identb']
        for kc in range(8):
            s.tr(trp[:, kc, :], hb[:, kc * 128:(kc + 1) * 128], identb[:], r=[hb, identb], w=[trp])
        s.copy('act', hT[:, :, hT_slice], trp[:], r=[trp], w=[hT])

    def norm_res(self, need_hb=True, ntrp=2):
        s = self.s
        eps = s.sb([128, 1], F32, 'eps')
        s.op('dve', lambda e: e.memset(eps[:], EPS), w=[eps])
        return dict(
            junk=s.sb([128, D], F32, 'junk'),
            ss=Rot([s.sb([128, 1], F32, 'ss') for _ in range(2)]),
            rstd=Rot([s.sb([128, 1], F32, 'rstd') for _ in range(2)]),
            tmp=s.sb([128, D], F32, 'tmp'),
            hb=Rot([s.sb([128, D], BF16, 'hb') for _ in range(3)]) if need_hb else None,
            trp=Rot([s.ps([128, 8, 128], BF16, 'trp') for _ in range(ntrp)]),
            identb=self.load_ident_bf16(),
            eps=eps,
        )

    def phase_proj(self, l):
        s = self.s
        I = self.I
        F = F0 if l == 0 else F1
        wsrc = I['even_w_in'][0] if l == 0 else I['odd_w_in'][0]
        xsrc = I['xin'] if l == 0 else self.S['XR']
        xdb = None if l == 0 else self.db['XR']
        with s.phase('proj%d' % l):
            pre = self.pre.get('proj%d' % l)
            if pre is not None and pre['loaded']:
                w = pre['w']
            else:
                w = s.sb([128, 8, F], BF16, 'win')
                self.load_w_bf16(w, wsrc, D, F)
            res = self.norm_res()
            A = [self.load_mod_bc(l, 0, sg, 'A1') for sg in range(2)]
            B = [self.load_mod_bc(l, 1, sg, 'B1') for sg in range(2)]
            xts = Rot([s.sb([128, D], F32, 'xt') for _ in range(3)])
            hTs = Rot([s.sb([128, 8, 128], BF16, 'hT') for _ in range(3)])
            pss = Rot([s.ps([128, 512], F32, 'pp') for _ in range(5)])
            if l == 0:
                outs = [(0, F0, 'P0', 0, BF16)]
                fmap = {1024: AF.Silu}
            else:
                outs = [(0, 1024, 'P1A', 0, BF16), (4096, 5120, 'P1A', 2048, BF16), (1024, 3072, 'P1B', 0, F32),
                        (3072, 4096, 'P1A', 1024, BF16)]
                fmap = {0: AF.Silu, 512: AF.Silu, 1024: AF.Sigmoid, 1536: AF.Sigmoid, 2048: AF.Sigmoid, 2560: AF.Sigmoid,
                        4096: AF.Silu, 4608: AF.Silu}
            stg = {}
            for (c0, c1, dn, d0, dt) in outs:
                stg[(c0, dn)] = Rot([s.sb([128, c1 - c0], dt, 'stg') for _ in range(2)])
            ev = Rot(['dve', 'act', 'dve'])
            ctx = {}
            if l == 1:
                LBa = s.sb([128, 2 * D], F32, 'LBa')
                OMLa = s.sb([128, 2 * D], F32, 'OMLa')
                for d_ in range(2):
                    s.dma('sp', OMLa[:, d_ * D:(d_ + 1) * D], bcast_row(I['hgrn_lower_bounds'][d_, 0:1, :]), w=[OMLa])
                    s.dma('sp', LBa[:, d_ * D:(d_ + 1) * D], bcast_row(I['hgrn_lower_bounds'][d_, 1:2, :]), w=[LBa])
                s.tt('dve', LBa[:], LBa[:], OMLa[:], ALU.subtract, r=[LBa, OMLa], w=[LBa])
                s.act(LBa[:], LBa[:], AF.Sigmoid, r=[LBa], w=[LBa])
                s.ts('dve', OMLa[:], LBa[:], -1.0, 1.0, ALU.mult, ALU.add, r=[LBa], w=[OMLa])

            def stA(t):
                seg = 1 if t < 2 else 0
                xt = xts.next()
                s.dma('sp', xt[:], xsrc[t * 128:(t + 1) * 128, :], r=[xdb[t]] if xdb else [], w=[xt])
                ctx[t] = self.norm_mod_T(xt, A[seg], B[seg], None, None, res)

            def stB(t):
                hT = hTs.next()
                self.transpose_T(ctx[t], hT, slice(0, 128), res)
                ctx[t] = hT

            def stC(t):
                hT = ctx.pop(t)
                for (c0, c1, dn, d0, dt) in outs:
                    st = stg[(c0, dn)].next()
                    for n0 in range(c0, c1, 512):
                        n1 = min(n0 + 512, c1)
                        p = pss.next()
                        for kc in range(8):
                            s.mm(p[:, 0:n1 - n0], hT[:, kc, :], w[:, kc, n0:n1], kc == 0, kc == 7, r=[hT, w], w=[p])
                        if n0 in fmap:
                            s.act(st[:, n0 - c0:n1 - c0], p[:, 0:n1 - n0], fmap[n0], r=[p], w=[st])
                            if fmap[n0] == AF.Sigmoid:
                                s.tt('dve', st[:, n0 - c0:n1 - c0], st[:, n0 - c0:n1 - c0], OMLa[:, n0 - c0:n1 - c0], ALU.mult,
                                     r=[st, OMLa], w=[st])
                                s.tt('dve', st[:, n0 - c0:n1 - c0], st[:, n0 - c0:n1 - c0], LBa[:, n0 - c0:n1 - c0], ALU.add,
                                     r=[st, LBa], w=[st])
                        else:
                            s.copy(ev.next(), st[:, n0 - c0:n1 - c0], p[:, 0:n1 - n0], r=[p], w=[st])
                    s.dma('sp', self.S[dn][t * 128:(t + 1) * 128, d0:d0 + (c1 - c0)], st[:], r=[st], w=[self.db[dn][t]])
            rA = [s.record(stA, t) for t in range(NT)]
            rB = [s.record(stB, t) for t in range(NT)]
            rC = [s.record(stC, t) for t in range(NT)]
            s.replay(rA[0])
            s.replay(rB[0])
            s.replay(rA[1])
            for t in range(NT):
                s.replay(rC[t])
                if t + 1 < NT:
                    s.replay(rB[t + 1])
                if t + 2 < NT:
                    s.replay(rA[t + 2])


def make_inputs(inputs, b, consts):
    x = np.asarray(inputs['x'])
    ctx = np.asarray(inputs['ctx'])
    m = {}
    m['xin'] = np.ascontiguousarray(np.concatenate([ctx[b], x[b]], axis=0), dtype=np.float32)
    cc = np.stack([np.asarray(inputs['c'])[b], np.asarray(inputs['c_ctx'])], axis=0)
    m['cc'] = np.ascontiguousarray(cc.reshape(2, 8, 128).transpose(2, 1, 0).reshape(128, 16), dtype=np.float32)
    for n in WEIGHT_NAMES:
        m[n] = np.ascontiguousarray(np.asarray(inputs[n]), dtype=np.float32)
    m.update(consts)
    return m


_PROG = None


def kernel(**inputs):
    global _PROG
    if _PROG is None:
        _PROG = Prog()
    consts = host_consts()
    in_maps = [make_inputs(inputs, b, consts) for b in range(8)]
    res = run_bass_kernel_spmd(_PROG.nc, in_maps, core_ids=list(range(8)))
    return np.stack([np.asarray(r['out']) for r in res.results], axis=0).astype(np.float32)
```
